# Optimizing a Trainium2 kernel written in Bass

```python
import jax
import jax.numpy as jnp
from jax import lax
import numpy as np

D_MODEL = 1024
BATCH = 4
SEQ = 8192
DEPTH = 2

GRID_W = 64
CTX_LEN = 256
N_GROUPS = 4
GROUP_W = D_MODEL // N_GROUPS
MIX_W = N_GROUPS * GROUP_W

GLA_HEADS = 4
GLA_DK = GROUP_W // (2 * GLA_HEADS)
GLA_DV = GROUP_W // GLA_HEADS
GLA_RANK = 16
GLA_GATE_NORMALIZER = 16.0
GLA_LOG_DECAY_MIN = -1.0
GLA_CHUNK = 64

POOL_WINDOWS = (2, 4, 8, 16)
POOL_CH = GROUP_W // len(POOL_WINDOWS)

SGU_CHUNK = 128
SGU_HEADS = 4
SGU_HD = GROUP_W // SGU_HEADS

CONV_WIDTH = 31
FFN_HIDDEN = 2816
FFN_CONV = 3

GLA_QK = GLA_HEADS * GLA_DK
COLS_A = 2 * GLA_QK + 2 * GROUP_W + 2 * GLA_RANK
COLS_B = GROUP_W
COLS_C = 2 * GROUP_W
COLS_D = 2 * GROUP_W
IN_COLS = COLS_A + COLS_B + COLS_C + COLS_D
MIX_SPLITS = (COLS_A, COLS_A + COLS_B, COLS_A + COLS_B + COLS_C)
GLA_SPLITS = (GLA_QK, 2 * GLA_QK, 2 * GLA_QK + GROUP_W, 2 * GLA_QK + 2 * GROUP_W,
              2 * GLA_QK + 2 * GROUP_W + GLA_RANK)

ALPHA = (2 * DEPTH) ** 0.25
BETA = (8 * DEPTH) ** -0.25
EPS = 1e-6

kernel_name = 'hybrid_parallel_mixer_dit_block'


def _norm(x):
    x32 = x.astype(jnp.float32)
    mu = jnp.mean(x32, axis=-1, keepdims=True)
    var = jnp.mean(jnp.square(x32 - mu), axis=-1, keepdims=True)
    return ((x32 - mu) * lax.rsqrt(var + EPS)).astype(x.dtype)


def layer_norm(x, w, b):
    return _norm(x) * w + b


def modulate(x, shift, scale):
    return _norm(x) * (1 + scale) + shift


def split_heads(t, h):
    b, l, _ = t.shape
    return t.reshape(b, l, h, -1).transpose(0, 2, 1, 3)


def merge_heads(t):
    b, h, l, d = t.shape
    return t.transpose(0, 2, 1, 3).reshape(b, l, h * d)


def _to_chunks(t):
    b, h, l, d = t.shape
    return t.reshape(b, h, l // GLA_CHUNK, GLA_CHUNK, d)


def gla_chunk_summaries(k, v, g):
    cum = jnp.cumsum(g, axis=3)
    cum_last = cum[:, :, :, -1, :]
    k_dec = k * jnp.exp(cum_last[:, :, :, None, :] - cum)
    kv = jnp.einsum('bhncd,bhnce->bhnde', k_dec, v)
    return cum, kv, jnp.exp(cum_last)


def gla_chunk_states(kv, decay, s0):
    def step(s, inp):
        kv_n, dec_n = inp
        return dec_n[..., None] * s + kv_n, s
    s_fin, s_start = lax.scan(step, s0, (jnp.moveaxis(kv, 2, 0), jnp.moveaxis(decay, 2, 0)))
    return jnp.moveaxis(s_start, 0, 2), s_fin


def gla_scan(q, k, v, g, s0):
    b, h, l, dv = v.shape
    q, k, v, g = (_to_chunks(t) for t in (q, k, v, g))
    cum, kv, decay = gla_chunk_summaries(k, v, g)
    s_start, s_fin = gla_chunk_states(kv, decay, s0)
    q_in = q * jnp.exp(cum)
    k_in = k * jnp.exp(-cum)
    tri = jnp.tril(jnp.ones((GLA_CHUNK, GLA_CHUNK), dtype=bool))
    a = jnp.where(tri, jnp.einsum('bhncd,bhnsd->bhncs', q_in, k_in), 0.0)
    o = jnp.einsum('bhncs,bhnse->bhnce', a, v) + jnp.einsum('bhncd,bhnde->bhnce', q_in, s_start)
    return o.reshape(b, h, l, dv), s_fin


def gla_final_state(k, v, g, s0):
    _, kv, decay = gla_chunk_summaries(_to_chunks(k), _to_chunks(v), _to_chunks(g))
    _, s_fin = gla_chunk_states(kv, decay, s0)
    return s_fin


def gla_log_decay(z_low, w_up, b_up):
    logit = z_low @ w_up + b_up
    return jnp.maximum(jax.nn.log_sigmoid(logit) / GLA_GATE_NORMALIZER, GLA_LOG_DECAY_MIN)


def gla_mixer(za_lat, za_ctx, w_gate, b_gate, norm_w, with_ctx_out):
    def prep(za):
        q, k, v, r, low_f, low_b = jnp.split(za.astype(jnp.float32), GLA_SPLITS, axis=-1)
        q = split_heads(q, GLA_HEADS) * GLA_DK ** -0.5
        k = split_heads(k, GLA_HEADS)
        v = split_heads(v, GLA_HEADS)
        g_f = split_heads(gla_log_decay(low_f, w_gate[0], b_gate[0]), GLA_HEADS)
        g_b = split_heads(gla_log_decay(low_b, w_gate[1], b_gate[1]), GLA_HEADS)
        return q, k, v, r, g_f, g_b

    def readout(o, r, dtype):
        o = o * lax.rsqrt(jnp.mean(jnp.square(o), axis=-1, keepdims=True) + EPS)
        return (merge_heads(o) * norm_w * jax.nn.silu(r)).astype(dtype)

    flip = lambda t: jnp.flip(t, axis=2)
    s0 = jnp.zeros((za_ctx.shape[0], GLA_HEADS, GLA_DK, GLA_DV), jnp.float32)
    qc, kc, vc, rc, gfc, gbc = prep(za_ctx)
    if with_ctx_out:
        oc_f, s_f = gla_scan(qc, kc, vc, gfc, s0)
        oc_b, s_b = gla_scan(flip(qc), flip(kc), flip(vc), flip(gbc), s0)
        y_ctx = readout(oc_f + flip(oc_b), rc, za_ctx.dtype)
    else:
        s_f = gla_final_state(kc, vc, gfc, s0)
        s_b = gla_final_state(flip(kc), flip(vc), flip(gbc), s0)
        y_ctx = None
    q, k, v, r, g_f, g_b = prep(za_lat)
    o_f, _ = gla_scan(q, k, v, g_f, s_f)
    o_b, _ = gla_scan(flip(q), flip(k), flip(v), flip(g_b), s_b)
    return readout(o_f + flip(o_b), r, za_lat.dtype), y_ctx


def pool_mixer(zb, pool_w, pool_scale):
    b, l, _ = zb.shape
    x32 = zb.astype(jnp.float32)
    cs = jnp.concatenate([jnp.zeros((b, 1, GROUP_W), jnp.float32), jnp.cumsum(x32, axis=1)], axis=1)
    t = jnp.arange(l)
    outs = []
    for gi, w in enumerate(POOL_WINDOWS):
        lo = jnp.clip(t - w // 2, 0, l)
        hi = jnp.clip(t + w - w // 2, 0, l)
        sl = slice(gi * POOL_CH, (gi + 1) * POOL_CH)
        mean = (cs[:, hi, sl] - cs[:, lo, sl]) / (hi - lo).astype(jnp.float32)[None, :, None]
        outs.append(mean - x32[:, :, sl])
    y = jnp.stack(outs, axis=2).astype(zb.dtype)
    y = jnp.einsum('blgc,gcd->blgd', y, pool_w).reshape(b, l, GROUP_W)
    return y * pool_scale


def sgu_mixer(zc, sgu_w, sgu_b, ln_w, ln_b):
    z = jax.nn.gelu(zc)
    u, v = jnp.split(z, 2, axis=-1)
    v = layer_norm(v, ln_w, ln_b)
    b, l, _ = v.shape
    vh = v.reshape(b, l // SGU_CHUNK, SGU_CHUNK, SGU_HEADS, SGU_HD)
    s = jnp.einsum('hts,bnshd->bnthd', sgu_w, vh) + sgu_b.T[:, :, None]
    return u * s.reshape(b, l, GROUP_W)


def conv_module(zd, conv_w, conv_b, ln_w, ln_b):
    a, g = jnp.split(zd, 2, axis=-1)
    y = a * jax.nn.sigmoid(g)
    y = lax.conv_general_dilated(y, conv_w[:, None, :], window_strides=(1,),
                                 padding=[(CONV_WIDTH // 2, CONV_WIDTH // 2)],
                                 dimension_numbers=('NWC', 'WIO', 'NWC'),
                                 feature_group_count=GROUP_W) + conv_b
    return jax.nn.silu(layer_norm(y, ln_w, ln_b))


def conv_ffn(h, w_up, conv_w, w_down, rows, width):
    b, l, _ = h.shape
    u = (h @ w_up).reshape(b, rows, width, 2 * FFN_HIDDEN)
    u = lax.conv_general_dilated(u, conv_w[:, :, None, :], window_strides=(1, 1), padding='SAME',
                                 dimension_numbers=('NHWC', 'HWIO', 'NHWC'),
                                 feature_group_count=2 * FFN_HIDDEN)
    a, g = jnp.split(u.reshape(b, l, 2 * FFN_HIDDEN), 2, axis=-1)
    return (jax.nn.silu(g) * a) @ w_down


def token_mixer(h_lat, h_ctx, w_in, gla_w_gate, gla_b_gate, gla_norm_w, pool_w, pool_scale,
                sgu_w, sgu_b, sgu_ln_w, sgu_ln_b, cm_conv_w, cm_conv_b, cm_ln_w, cm_ln_b, w_out,
                with_ctx_out):
    def other_groups(z):
        _, zb, zc, zd = jnp.split(z, MIX_SPLITS, axis=-1)
        return [pool_mixer(zb, pool_w, pool_scale),
                sgu_mixer(zc, sgu_w, sgu_b, sgu_ln_w, sgu_ln_b),
                conv_module(zd, cm_conv_w, cm_conv_b, cm_ln_w, cm_ln_b)]
    z_lat = h_lat @ w_in
    z_ctx = h_ctx @ (w_in if with_ctx_out else w_in[:, :COLS_A])
    ya_lat, ya_ctx = gla_mixer(z_lat[..., :COLS_A], z_ctx[..., :COLS_A],
                               gla_w_gate, gla_b_gate, gla_norm_w, with_ctx_out)
    y_lat = jnp.concatenate([ya_lat] + other_groups(z_lat), axis=-1) @ w_out
    y_ctx = (jnp.concatenate([ya_ctx] + other_groups(z_ctx), axis=-1) @ w_out) if with_ctx_out else None
    return y_lat, y_ctx


def setup_inputs(seed: int = 0) -> dict:
    key = jax.random.key(seed)
    ks = jax.random.split(key, 26)

    def nrm(k, shape, scale):
        return jax.random.normal(k, shape, jnp.float32) * scale

    return {
        'x': nrm(ks[0], (BATCH, SEQ, D_MODEL), 1.0),
        'c': nrm(ks[1], (BATCH, D_MODEL), 1.0),
        'ctx': nrm(ks[2], (BATCH, CTX_LEN, D_MODEL), 1.0),
        'c_ctx': nrm(ks[3], (D_MODEL,), 1.0),
        'w_mod': nrm(ks[4], (DEPTH, D_MODEL, 6 * D_MODEL), 0.5 * D_MODEL ** -0.5),
        'b_mod': nrm(ks[5], (DEPTH, 6 * D_MODEL), 0.02),
        'w_in': nrm(ks[6], (DEPTH, D_MODEL, IN_COLS), D_MODEL ** -0.5),
        'gla_w_gate': nrm(ks[7], (DEPTH, 2, GLA_RANK, GLA_QK), GLA_RANK ** -0.5),
        'gla_b_gate': nrm(ks[8], (DEPTH, 2, GLA_QK), 0.02),
        'gla_norm_w': 1.0 + nrm(ks[9], (DEPTH, GROUP_W), 0.02),
        'pool_w': nrm(ks[10], (DEPTH, len(POOL_WINDOWS), POOL_CH, POOL_CH), POOL_CH ** -0.5),
        'pool_scale': 1.0 + nrm(ks[11], (DEPTH, GROUP_W), 0.02),
        'sgu_w': nrm(ks[12], (DEPTH, SGU_HEADS, SGU_CHUNK, SGU_CHUNK), SGU_CHUNK ** -0.5),
        'sgu_b': 1.0 + nrm(ks[13], (DEPTH, SGU_HEADS, SGU_CHUNK), 0.02),
        'sgu_ln_w': 1.0 + nrm(ks[14], (DEPTH, GROUP_W), 0.02),
        'sgu_ln_b': nrm(ks[15], (DEPTH, GROUP_W), 0.02),
        'cm_conv_w': nrm(ks[16], (DEPTH, CONV_WIDTH, GROUP_W), CONV_WIDTH ** -0.5),
        'cm_conv_b': nrm(ks[17], (DEPTH, GROUP_W), 0.02),
        'cm_ln_w': 1.0 + nrm(ks[18], (DEPTH, GROUP_W), 0.02),
        'cm_ln_b': nrm(ks[19], (DEPTH, GROUP_W), 0.02),
        'w_out': nrm(ks[20], (DEPTH, MIX_W, D_MODEL), BETA * MIX_W ** -0.5),
        'ffn_w_up': nrm(ks[21], (DEPTH, D_MODEL, 2 * FFN_HIDDEN), D_MODEL ** -0.5),
        'ffn_conv_w': nrm(ks[22], (DEPTH, FFN_CONV, FFN_CONV, 2 * FFN_HIDDEN), 1.0 / FFN_CONV),
        'ffn_w_down': nrm(ks[23], (DEPTH, FFN_HIDDEN, D_MODEL), BETA * FFN_HIDDEN ** -0.5),
        'post_ln_w': 1.0 + nrm(ks[24], (DEPTH, 2, D_MODEL), 0.02),
        'post_ln_b': nrm(ks[25], (DEPTH, 2, D_MODEL), 0.02),
    }


def reference(x, c, ctx, c_ctx, w_mod, b_mod, w_in, gla_w_gate, gla_b_gate, gla_norm_w,
              pool_w, pool_scale, sgu_w, sgu_b, sgu_ln_w, sgu_ln_b, cm_conv_w, cm_conv_b,
              cm_ln_w, cm_ln_b, w_out, ffn_w_up, ffn_conv_w, ffn_w_down, post_ln_w, post_ln_b):
    rows = x.shape[1] // GRID_W
    ctx_len = ctx.shape[1]
    sc = jax.nn.silu(c)
    sc_ctx = jax.nn.silu(c_ctx)
    for l in range(DEPTH):
        with_ctx_out = l < DEPTH - 1
        m_lat = jnp.split((sc @ w_mod[l] + b_mod[l])[:, None, :], 6, axis=-1)
        m_ctx = jnp.split(sc_ctx @ w_mod[l] + b_mod[l], 6, axis=-1)
        h_lat = modulate(x, m_lat[0], m_lat[1])
        h_ctx = modulate(ctx, m_ctx[0], m_ctx[1])
        y_lat, y_ctx = token_mixer(h_lat, h_ctx, w_in[l], gla_w_gate[l], gla_b_gate[l], gla_norm_w[l],
                                   pool_w[l], pool_scale[l], sgu_w[l], sgu_b[l], sgu_ln_w[l], sgu_ln_b[l],
                                   cm_conv_w[l], cm_conv_b[l], cm_ln_w[l], cm_ln_b[l], w_out[l],
                                   with_ctx_out)
        x = layer_norm(ALPHA * x + m_lat[2] * y_lat, post_ln_w[l, 0], post_ln_b[l, 0])
        h_lat = modulate(x, m_lat[3], m_lat[4])
        f_lat = conv_ffn(h_lat, ffn_w_up[l], ffn_conv_w[l], ffn_w_down[l], rows, GRID_W)
        x = layer_norm(ALPHA * x + m_lat[5] * f_lat, post_ln_w[l, 1], post_ln_b[l, 1])
        if with_ctx_out:
            ctx = layer_norm(ALPHA * ctx + m_ctx[2] * y_ctx, post_ln_w[l, 0], post_ln_b[l, 0])
            h_ctx = modulate(ctx, m_ctx[3], m_ctx[4])
            f_ctx = conv_ffn(h_ctx, ffn_w_up[l], ffn_conv_w[l], ffn_w_down[l], 1, ctx_len)
            ctx = layer_norm(ALPHA * ctx + m_ctx[5] * f_ctx, post_ln_w[l, 1], post_ln_b[l, 1])
    return x
```

```python
import contextlib
import numpy as np
import concourse.bass as bass
import concourse.mybir as mybir
from concourse.bass_utils import run_bass_kernel_spmd

F32 = mybir.dt.float32
BF16 = mybir.dt.bfloat16
AF = mybir.ActivationFunctionType
ALU = mybir.AluOpType
AX = mybir.AxisListType

PE, ACT, DVE, POOL, SP = "pe", "act", "dve", "pool", "sp"
ENGS = (PE, ACT, DVE, POOL, SP)

D = 1024
GW = 64
LC = 256
DEPTH = 2
HID = 2816
NPAIR = 22
ALPHA = (2 * DEPTH) ** 0.25
EPS = 1e-6
NZT = 896
NZF = 1312
NWIN = NZT + NZF


class Res:
    __slots__ = ("w", "r")

    def __init__(self):
        self.w = None
        self.r = {}


class Prog:
    def __init__(self, nc):
        self.nc = nc
        self.ops = {e: [] for e in ENGS}
        self.sems = {}
        self.count = {}
        self.waited = {e: {} for e in ENGS}
        for e in ENGS:
            self.sems[e] = nc.alloc_semaphore(name="s_" + e)
            self.count[e] = 0
        self.nd = 0

    def dsem(self):
        k = "d%d" % self.nd
        self.nd += 1
        self.sems[k] = self.nc.alloc_semaphore(name="s_" + k)
        self.count[k] = 0
        return k

    def _need(self, eng, tok, waits):
        if tok is None:
            return
        k, v = tok
        if k == PE and eng == PE:
            return
        if k not in ENGS:
            v = max(v, self.count[k])
        if self.waited[eng].get(k, 0) >= v:
            return
        if waits.get(k, 0) < v:
            waits[k] = v

    def op(self, eng, fn, reads=(), writes=(), flag=True, dsem=None):
        waits = {}
        for r in reads:
            self._need(eng, r.w, waits)
        for w in writes:
            self._need(eng, w.w, waits)
            for t in w.r.items():
                self._need(eng, t, waits)
        for k, v in waits.items():
            self.waited[eng][k] = v
        if dsem is not None:
            self.count[dsem] += 16
            tok = (dsem, self.count[dsem])
            inc = (dsem, 16)
        elif flag:
            self.count[eng] += 1
            tok = (eng, self.count[eng])
            inc = (eng, 1)
        else:
            tok = (eng, self.count[eng] + 1)
            inc = None
        for r in reads:
            if r.r.get(tok[0], 0) < tok[1]:
                r.r[tok[0]] = tok[1]
        for w in writes:
            w.w = tok
            w.r = {}
        self.ops[eng].append((list(waits.items()), fn, inc))

    def barrier(self):
        tot = [(k, c) for k, c in self.count.items() if c > 0]
        for e in ENGS:
            w = [(k, c) for k, c in tot if self.waited[e].get(k, 0) < c]
            for k, c in w:
                self.waited[e][k] = c
            self.ops[e].append((w, None, None))

    def emit(self, final=False):
        nc = self.nc
        engmap = {PE: "tensor", ACT: "scalar", DVE: "vector", POOL: "gpsimd", SP: "sync"}
        endw = [(k, c) for k, c in self.count.items() if c > 0]
        with nc.Block() as block:
            for e in ENGS:
                ops = self.ops[e]
                if final and e == SP:
                    ops = ops + [(endw, None, None)]
                if not ops:
                    continue

                def body(engine, ops=ops):
                    for waits, fn, inc in ops:
                        for k, v in waits:
                            engine.wait_ge(self.sems[k], v)
                        if fn is None:
                            continue
                        ins = fn(engine)
                        if inc is not None:
                            ins.then_inc(self.sems[inc[0]], inc[1])

                getattr(block, engmap[e])(body)
        self.ops = {e: [] for e in ENGS}


class Ring:
    def __init__(self, K, name, shape, dt, n, dma=False, psum=False):
        self.t = []
        self.r = []
        self.d = []
        self.i = 0
        for j in range(n):
            if psum:
                self.t.append(K.ps("%s%d" % (name, j), dt))
            else:
                self.t.append(K.sb("%s%d" % (name, j), shape, dt))
            self.r.append(Res())
            self.d.append(K.P.dsem() if dma else None)

    def next(self):
        j = self.i % len(self.t)
        self.i += 1
        return self.t[j], self.r[j], self.d[j]


class StopBuild(Exception):
    pass


class K:
    stop = None

    def __init__(self, L, depth=DEPTH, R=8):
        self.L = L
        self.depth = depth
        self.R = R
        self.nc = bass.Bass("TRN2", target_bir_lowering=False)
        self.P = Prog(self.nc)
        self.inp = {}
        self.uid = 0
        self.scopes = []

    @contextlib.contextmanager
    def scope(self):
        with contextlib.ExitStack() as es:
            self.scopes.append(es)
            try:
                yield
            finally:
                self.scopes.pop()

    def chk(self, tag):
        if self.stop == tag:
            self.P.barrier()
            self.P.emit(final=True)
            raise StopBuild()

    def din(self, name, shape, dt=F32):
        self.inp[name] = (tuple(shape), dt)
        return self.nc.dram_tensor(name, list(shape), dt, kind="ExternalInput").ap()

    def dscr(self, name, shape, dt):
        kind = "ExternalOutput" if name in DEBUG else "Internal"
        return self.nc.dram_tensor(name, list(shape), dt, kind=kind).ap()

    def sb(self, name, shape, dt):
        self.uid += 1
        return self.scopes[-1].enter_context(self.nc.sbuf_tensor("%s_%d" % (name, self.uid), list(shape), dt))

    def ps(self, name, dt=F32):
        self.uid += 1
        return self.scopes[-1].enter_context(
            self.nc.psum_tensor("%s_%d" % (name, self.uid), [128, 512 if dt == F32 else 1024], dt))

    def op(self, *a, **k):
        self.P.op(*a, **k)

    def load(self, q, dst, src, res, dsem):
        self.op(q, lambda e: e.dma_start(out=dst, in_=src), writes=[res], dsem=dsem)

    def const(self, name, shape, dt=F32, cast=None, q=SP):
        src = self.din(name, shape)
        t = self.sb(name, shape, cast or dt)
        r = Res()
        nfree = int(np.prod(shape[1:]))
        if cast and nfree >= 4096:
            names = "abcd"[:len(shape) - 1]
            pat = "p %s -> p (%s)" % (" ".join(names), " ".join(names))
            tf = t[:].rearrange(pat) if len(shape) > 2 else t[:]
            sf = src.rearrange(pat) if len(shape) > 2 else src
            self.cast_cols(tf, sf, nfree, r)
        else:
            self.load(POOL if cast else q, t[:], src, r, self.cd)
        return t, r

    def cast_cols(self, dst2d, src2d, n, rdst):
        c = 0
        while c < n:
            w = min(1024, n - c)
            st, rs, ds = self.cst.next()
            self.op(SP, lambda e, st=st, c=c, w=w: e.dma_start(out=st[:, 0:w], in_=src2d[:, c:c + w]), writes=[rs],
                    dsem=ds)
            eng = ACT if (self.cast_i % 2 == 0) else DVE
            self.cast_i += 1
            if eng == ACT:
                self.op(ACT, lambda e, st=st, c=c, w=w: e.activation(out=dst2d[:, c:c + w], in_=st[:, 0:w],
                                                                     func=AF.Copy), reads=[rs], writes=[rdst])
            else:
                self.op(DVE, lambda e, st=st, c=c, w=w: e.tensor_copy(out=dst2d[:, c:c + w], in_=st[:, 0:w]),
                        reads=[rs], writes=[rdst])
            c += w

    def pipeline(self, nt, prefetch, tilegen, depth=2, lag=0):
        active = []
        nxt = 0
        while nxt < nt or active:
            while len(active) < depth and nxt < nt and (not active or active[-1][1] >= lag):
                prefetch(nxt)
                active.append([tilegen(nxt), 0])
                nxt += 1
            for ent in list(active):
                try:
                    next(ent[0])
                    ent[1] += 1
                except StopIteration:
                    active.remove(ent)

    def ln_stats(self, src, n, rsrc):
        nchk = max(1, n // 512)
        w = n // nchk
        st, rst, _ = self.r_st.next()
        mv, rmv, _ = self.r_mv.next()
        for c in range(nchk):
            self.op(DVE, lambda e, c=c: e.bn_stats(out=st[:, c * 6:(c + 1) * 6], in_=src[:, c * w:(c + 1) * w]),
                    reads=[rsrc], writes=[rst])
        self.op(DVE, lambda e: e.bn_aggr(out=mv[:, 0:2], in_=st[:, 0:nchk * 6]), reads=[rst], writes=[rmv])
        self.op(ACT, lambda e: e.activation(out=mv[:, 2:3], in_=mv[:, 1:2], func=AF.Sqrt, bias=EPS, scale=1.0),
                reads=[rmv], writes=[rmv])
        self.op(DVE, lambda e: e.reciprocal(out=mv[:, 3:4], in_=mv[:, 2:3]), reads=[rmv], writes=[rmv])
        self.op(DVE, lambda e: e.scalar_tensor_tensor(out=mv[:, 4:5], in0=mv[:, 0:1], scalar=-1.0, in1=mv[:, 3:4],
                                                      op0=ALU.mult, op1=ALU.mult), reads=[rmv], writes=[rmv])
        return mv[:, 3:4], mv[:, 4:5], rmv

    def build(self):
        self._root = contextlib.ExitStack()
        self.scopes.append(self._root)
        return self._build()

    def _build(self):
        nc, P, L = self.nc, self.P, self.L
        self.cd = P.dsem()
        x_in = self.din("x", [L, D])
        c_in = self.din("ctx", [LC, D])
        self.out = nc.dram_tensor("out", [L, D], F32, kind="ExternalOutput").ap()
        self.XS = self.dscr("XS", [L, D], F32)
        self.CS = self.dscr("CS", [LC, D], F32)
        self.X1 = {"lat": self.dscr("X1l", [L, D], F32), "ctx": self.dscr("X1c", [LC, D], F32)}
        self.H2T = {"lat": self.dscr("H2l", [8, 128, L], BF16), "ctx": self.dscr("H2c", [8, 128, LC], BF16)}
        self.ZT = {"lat": self.dscr("ZTl", [L, NZT], BF16), "ctx": self.dscr("ZTc", [LC, NZT], BF16)}
        self.ZF = {"lat": self.dscr("ZFl", [9, 128, L], BF16), "ctx": self.dscr("ZFc", [9, 128, LC], BF16)}
        self.G = {"lat": self.dscr("Gl", [L, 256], F32), "ctx": self.dscr("Gc", [LC, 256], F32)}
        self.WUPB = self.dscr("WUPB", [NPAIR, 128, 8 * 256], BF16)
        if "YD" in DEBUG:
            self.YD = self.dscr("YD", [8, 128, L], BF16)
            self.r_YD = Res()
            self.ydsem = P.dsem()
        self.r_dram = {k: Res() for k in ("XS", "CS", "X1lat", "X1ctx", "H2lat", "H2ctx", "ZTlat", "ZTctx",
                                           "ZFlat", "ZFctx", "Glat", "Gctx", "WUPB")}
        self.identb, self.r_identb = self.const("identf", [128, 128], cast=BF16)
        self.mcum, self.r_mcum = self.const("mcum", [128, 2, 512])
        self.mrev, self.r_mrev = self.const("mrev", [128, 2, 128])
        self.ind, self.r_ind = self.const("ind", [128, 64])
        self.hm, self.r_hm = self.const("hm", [128, 4])
        self.bm, self.r_bm = self.const("bm", [128, 512])
        self.onesf, self.r_onesf = self.const("onesf", [128, 128])
        self.pbm, self.r_pbm = self.const("pbm", [128, 3, 4, 128], cast=BF16)
        self.pbp, self.r_pbp = self.const("pbp", [128, 4, 128], cast=BF16)
        self.pbn, self.r_pbn = self.const("pbn", [128, 4, 128], cast=BF16)
        self.cT, self.r_cT = self.const("cT", [128, 8, 2])
        self.r_consts = [self.r_identb, self.r_mcum, self.r_mrev, self.r_ind, self.r_hm, self.r_bm, self.r_onesf,
                         self.r_pbm, self.r_pbp, self.r_pbn]
        self.scT = self.sb("scT", [128, 8, 2], F32)
        self.r_scT = Res()
        self.op(ACT, lambda e: e.activation(out=self.scT[:], in_=self.cT[:], func=AF.Silu), reads=[self.r_cT],
                writes=[self.r_scT])
        self.screp = self.sb("screp", [128, 2, 8, 128], F32)
        self.r_screp = Res()
        for s in range(2):
            for kc in range(8):
                self.op(ACT, lambda e, s=s, kc=kc: e.activation(out=self.screp[:, s, kc, :], in_=self.onesf[:],
                                                               func=AF.Copy, scale=self.scT[:, kc, s:s + 1]),
                        reads=[self.r_onesf, self.r_scT], writes=[self.r_screp])
        self.mT = self.sb("mT", [128, 2, 6, 8], F32)
        self.r_mT = Res()
        self.grow = self.sb("grow", [128, 2, 2, D], F32)
        self.r_grow = Res()
        self.cst = Ring(self, "cst", [128, 1024], F32, 2, dma=True)
        self.cast_i = 0
        self.r_st = Ring(self, "st", [128, 12], F32, 4)
        self.r_mv = Ring(self, "mv", [128, 8], F32, 4)
        nchl, nchc = L // 64, LC // 64
        self.nch = {"ctx": nchc, "lat": nchl}
        self.choff = {"ctx": 0, "lat": nchc}
        self.NCH = nchl + nchc
        self.r_SF, self.r_SB, self.r_KVB, self.r_stf = Res(), Res(), Res(), Res()
        self.prow = self.sb("prow", [128, 4, D], F32)
        self.r_prow = Res()
        dm = self.sb("dm", [128, 8], F32)
        r_dm = Res()
        self.op(POOL, lambda e: e.memset(dm[:], 1.0), writes=[r_dm])
        for bv in (0.0, EPS, 1.0):
            self.op(ACT, lambda e, bv=bv: e.activation(out=dm[:, 4:8], in_=dm[:, 0:4], func=AF.Identity, bias=bv,
                                                       scale=1.0), reads=[r_dm], writes=[r_dm])
        P.emit()
        xsrc, csrc = x_in, c_in
        try:
            for l in range(self.depth):
                last = (l == self.depth - 1)
                self.layer(l, xsrc, csrc, self.out if last else self.XS, self.CS, last)
                xsrc, csrc = self.XS, self.CS
        except StopBuild:
            pass
        return nc

    def layer(self, l, xsrc, csrc, xdst, cdst, last):
        nc, P = self.nc, self.P
        pre = "l%d_" % l
        NCH = self.NCH
        with self.scope():
            wst = Ring(self, pre + "wst", [128, 8, D], F32, 2, dma=True)
            bmodT, r_bmodT = self.const(pre + "bmodT", [128, 6, 8])
            brow, r_brow = self.const(pre + "brow", [128, 2, D])
            self.load(SP, self.prow[:], self.din(pre + "prow", [128, 4, D]), self.r_prow, self.cd)
            pm = self.ps(pre + "pm")
            r_pm = Res()
            pg = [self.ps(pre + "pg0"), self.ps(pre + "pg1")]
            r_pg = [Res(), Res()]
            wmod = self.din(pre + "wmod", [6, 128, 8 * D])
            for v in range(6):
                w, rw, dw = wst.next()
                self.load(SP, w[:].rearrange("p a b -> p (a b)"), wmod[v], rw, dw)
                if v in (2, 5):
                    gi = 0 if v == 2 else 1
                    for s in range(2):
                        for nh in range(2):
                            for kc in range(8):
                                self.op(PE, lambda e, s=s, nh=nh, kc=kc, w=w: e.matmul(
                                    pg[nh][:, :], lhsT=self.screp[:, s, kc, :], rhs=w[:, kc, nh * 512:(nh + 1) * 512],
                                    start=(kc == 0), stop=(kc == 7)), reads=[rw, self.r_screp], writes=[r_pg[nh]],
                                    flag=(kc == 7))
                            self.op(DVE, lambda e, s=s, nh=nh, gi=gi: e.tensor_tensor(
                                out=self.grow[:, s, gi, nh * 512:(nh + 1) * 512], in0=pg[nh][:, :],
                                in1=brow[:, gi, nh * 512:(nh + 1) * 512], op=ALU.add),
                                reads=[r_pg[nh], r_brow], writes=[self.r_grow])
                else:
                    for j in range(8):
                        for kc in range(8):
                            self.op(PE, lambda e, j=j, kc=kc, w=w: e.matmul(
                                pm[:, 2 * j:2 * j + 2], lhsT=w[:, kc, j * 128:(j + 1) * 128], rhs=self.scT[:, kc, :],
                                start=(kc == 0), stop=(kc == 7)), reads=[rw, self.r_scT], writes=[r_pm],
                                flag=(kc == 7))
                    for s in range(2):
                        self.op(DVE, lambda e, s=s, v=v: e.tensor_tensor(
                            out=self.mT[:, s, v, :], in0=pm[:, 0:16].rearrange("p (j s) -> p s j", s=2)[:, s, :],
                            in1=bmodT[:, v, :], op=ALU.add), reads=[r_pm, r_bmodT], writes=[self.r_mT])
                        if v in (1, 4):
                            self.op(DVE, lambda e, s=s, v=v: e.tensor_scalar(
                                out=self.mT[:, s, v, :], in0=self.mT[:, s, v, :], scalar1=1.0, scalar2=None,
                                op0=ALU.add), reads=[self.r_mT], writes=[self.r_mT])
            wupf = self.din(pre + "wup", [NPAIR, 128, 8 * 256])
            stg = Ring(self, pre + "stg", [128, 8 * 256], BF16, 2, dma=True)
            sd = P.dsem()
            for j in range(NPAIR):
                t, rt, dt_ = stg.next()
                self.cast_cols(t[:], wupf[j], 8 * 256, rt)
                self.op(SP, lambda e, j=j, t=t: e.dma_start(out=self.WUPB[j], in_=t[:]), reads=[rt],
                        writes=[self.r_dram["WUPB"]], dsem=sd)
            P.barrier()
            P.emit()
            self.chk("S%d" % l)
        W = dict(prow=self.prow, r_prow=self.r_prow)
        with self.scope():
            self.SF = self.sb("SF", [128, NCH, 64], BF16)
            self.SB_ = self.sb("SB", [128, NCH, 64], BF16)
            with self.scope():
                W["win"], W["r_win"] = self.const(pre + "win", [128, 8, NWIN], cast=BF16)
                W["wg"], W["r_wg"] = self.const(pre + "wg", [33, 256], cast=BF16)
                W["lnrow"], W["r_lnrow"] = self.const(pre + "lnrow", [128, 2, 256])
                self.KVB = self.sb("KVB", [128, NCH, 64], F32)
                self.DECB = self.sb("DECB", [128, NCH], F32)
                self.stf = self.sb("stf", [128, 2, 64], F32)
                self.chain_i = 0
                self.op(POOL, lambda e: e.memset(self.stf[:, 0, :], 0.0), writes=[self.r_stf])
                BA = self.bufsA(pre)
                self.passA(BA, W, "ctx", csrc, LC)
                self.passA(BA, W, "lat", xsrc, self.L)
                self.bscan(pre)
                P.barrier()
                P.emit()
                self.chk("A%d" % l)
            with self.scope():
                W["wout"], W["r_wout"] = self.const(pre + "wout", [128, 8, D], cast=BF16)
                W["pw"], W["r_pw"] = self.const(pre + "pw", [128, 2, 128], cast=BF16)
                W["sw"], W["r_sw"] = self.const(pre + "sw", [128, 4, 128], cast=BF16)
                W["sgub"], W["r_sgub"] = self.const(pre + "sgub", [1, 512])
                W["convb"], W["r_convb"] = self.const(pre + "convb", [1, 256])
                W["fvec"], W["r_fvec"] = self.const(pre + "fvec", [128, 8])
                cw31, r_cw31 = self.const(pre + "cw31", [128, 2, 31])
                d31 = self.sb(pre + "d31", [128, 2, 31, 128], BF16)
                r_d31 = Res()
                for b in range(2):
                    for j in range(31):
                        self.op(POOL, lambda e, b=b, j=j: e.tensor_scalar(out=d31[:, b, j, :], in0=self.identb[:],
                                                                         scalar1=cw31[:, b, j:j + 1], scalar2=None,
                                                                         op0=ALU.mult),
                                reads=[self.r_identb, r_cw31], writes=[r_d31])
                W["d31"], W["r_d31"] = d31, r_d31
                BB = self.bufsB(pre)
                if not last:
                    self.passB(BB, W, "ctx", csrc, LC)
                self.passB(BB, W, "lat", xsrc, self.L)
                P.barrier()
                P.emit()
                self.chk("B%d" % l)
        with self.scope():
            W["wdn"], W["r_wdn"] = self.const(pre + "wdn", [128, NPAIR, D], cast=BF16)
            W["cw9"], W["r_cw9"] = self.const(pre + "cw9", [128, 44, 9])
            BC = self.bufsC(pre)
            if not last:
                self.passC(BC, W, "ctx", cdst, 1, LC, 1)
            self.passC(BC, W, "lat", xdst, self.L // GW, GW, self.R)
            if not last:
                self.chk("C%d" % l)
            P.barrier()
            P.emit(final=last)

    def bufsA(self, pre):
        B = {}
        p2 = pre + "A_"
        B["xt"] = Ring(self, p2 + "xt", [128, D], F32, 2, dma=True)
        B["xn"] = Ring(self, p2 + "xn", [128, D], BF16, 2)
        B["hT"] = Ring(self, p2 + "hT", [128, 8, 128], BF16, 2)
        B["zt"] = Ring(self, p2 + "zt", [128, NZT], BF16, 2, dma=True)
        B["zf"] = Ring(self, p2 + "zf", [128, 9, 128], BF16, 2, dma=True)
        B["gv"] = Ring(self, p2 + "gv", [128, 256], F32, 2)
        B["gv2"] = Ring(self, p2 + "gv2", [128, 256], F32, 1)
        B["sg"] = Ring(self, p2 + "sg", [128, 2, 128], F32, 1)
        B["ee"] = Ring(self, p2 + "ee", [128, 256], F32, 1)
        B["g"] = Ring(self, p2 + "g", [128, 256], F32, 2, dma=True)
        B["e3"] = Ring(self, p2 + "e3", [128, 256], F32, 2)
        B["kdec"] = Ring(self, p2 + "kdec", [128, 2, 2, 128], BF16, 2)
        B["dec"] = Ring(self, p2 + "dec", [128, 4], F32, 2)
        B["kvm"] = Ring(self, p2 + "kvm", [128, 512], F32, 1)
        B["kvc"] = Ring(self, p2 + "kvc", [128, 2, 2, 64], F32, 2)
        B["TB"] = (self.ps(p2 + "TB", BF16), Res())
        for nm in ("Z0", "Z1", "F0", "F1", "F2", "M0", "M1"):
            B[nm] = (self.ps(p2 + nm), Res())
        for t_, r_ in zip(B["zf"].t, B["zf"].r):
            self.op(POOL, lambda e, t_=t_: e.memset(t_[32:33, 8, :], 1.0), writes=[r_])
        return B

    def passA(self, B, W, sq, xsrc, Ls):
        si = 0 if sq == "lat" else 1
        nt = Ls // 128
        win, r_win = W["win"], W["r_win"]
        TB, r_TB = B["TB"]
        Fb = [B["F0"], B["F1"], B["F2"]]
        xts = {}

        def ldx(i):
            xt, r_xt, d_xt = B["xt"].next()
            self.load(SP, xt[:], xsrc[i * 128:(i + 1) * 128, :], r_xt, d_xt)
            xts[i] = (xt, r_xt)

        def tileA(i):
                xt, r_xt = xts.pop(i)
                rstd, nb, r_mv = self.ln_stats(xt, D, r_xt)
                xn, r_xn, _ = B["xn"].next()
                self.op(ACT, lambda e, xn=xn, xt=xt, rstd=rstd, nb=nb: e.activation(
                    out=xn[:], in_=xt[:], func=AF.Identity, scale=rstd, bias=nb), reads=[r_xt, r_mv], writes=[r_xn])
                for kc in range(8):
                    self.op(PE, lambda e, kc=kc, xn=xn: e.transpose(TB[:, kc * 128:(kc + 1) * 128],
                                                                    xn[:, kc * 128:(kc + 1) * 128], self.identb[:]),
                            reads=[r_xn, self.r_identb], writes=[r_TB], flag=(kc == 7))
                hT, r_hT, _ = B["hT"].next()
                for kc in range(8):
                    if kc % 2 == 0:
                        self.op(ACT, lambda e, kc=kc, hT=hT: e.activation(
                            out=hT[:, kc, :], in_=TB[:, kc * 128:(kc + 1) * 128], func=AF.Identity,
                            scale=self.mT[:, si, 1, kc:kc + 1], bias=self.mT[:, si, 0, kc:kc + 1]),
                            reads=[r_TB, self.r_mT], writes=[r_hT])
                    else:
                        self.op(DVE, lambda e, kc=kc, hT=hT: e.tensor_scalar(
                            out=hT[:, kc, :], in0=TB[:, kc * 128:(kc + 1) * 128], scalar1=self.mT[:, si, 1, kc:kc + 1],
                            scalar2=self.mT[:, si, 0, kc:kc + 1], op0=ALU.mult, op1=ALU.add),
                            reads=[r_TB, self.r_mT], writes=[r_hT])
                self.chk("a1")
                yield
                (Z0, r_Z0), (Z1, r_Z1) = B["Z0"], B["Z1"]
                for (Z, rZ, c0, cn) in ((Z0, r_Z0, 0, 512), (Z1, r_Z1, 512, NZT - 512)):
                    for kc in range(8):
                        self.op(PE, lambda e, kc=kc, Z=Z, c0=c0, cn=cn, hT=hT: e.matmul(
                            Z[:, 0:cn], lhsT=hT[:, kc, :], rhs=win[:, kc, c0:c0 + cn], start=(kc == 0), stop=(kc == 7)),
                            reads=[r_hT, r_win], writes=[rZ], flag=(kc == 7))
                zt, r_zt, d_zt = B["zt"].next()
                self.op(ACT, lambda e, zt=zt: e.activation(out=zt[:, 0:512], in_=Z0[:, 0:512], func=AF.Copy),
                        reads=[r_Z0], writes=[r_zt])
                self.op(DVE, lambda e, zt=zt: e.tensor_copy(out=zt[:, 512:640], in_=Z1[:, 0:128]), reads=[r_Z1],
                        writes=[r_zt])
                gv, r_gv, _ = B["gv"].next()
                self.op(ACT, lambda e, gv=gv: e.activation(out=gv[:], in_=Z1[:, 128:384], func=AF.Gelu_apprx_tanh),
                        reads=[r_Z1], writes=[r_gv])
                rstd2, nb2, r_mv2 = self.ln_stats(gv, 256, r_gv)
                gv2, r_gv2, _ = B["gv2"].next()
                self.op(ACT, lambda e, gv=gv, gv2=gv2, rstd2=rstd2, nb2=nb2: e.activation(
                    out=gv2[:], in_=gv[:], func=AF.Identity, scale=rstd2, bias=nb2), reads=[r_gv, r_mv2], writes=[r_gv2])
                self.op(DVE, lambda e, gv2=gv2: e.tensor_tensor(out=gv2[:], in0=gv2[:], in1=W["lnrow"][:, 0, :],
                                                                op=ALU.mult), reads=[r_gv2, W["r_lnrow"]], writes=[r_gv2])
                self.op(DVE, lambda e, gv2=gv2, zt=zt: e.tensor_tensor(out=zt[:, 640:896], in0=gv2[:],
                                                                       in1=W["lnrow"][:, 1, :], op=ALU.add),
                        reads=[r_gv2, W["r_lnrow"]], writes=[r_zt])
                self.op(SP, lambda e, zt=zt, i=i: e.dma_start(out=self.ZT[sq][i * 128:(i + 1) * 128, :], in_=zt[:]),
                        reads=[r_zt], writes=[self.r_dram["ZT" + sq]], dsem=d_zt)
                self.chk("a2")
                yield
                for blk in range(11):
                    Fp, rF = Fb[blk // 4]
                    m = 32 if blk == 10 else 128
                    c0 = NZT + blk * 128
                    for kc in range(8):
                        self.op(PE, lambda e, kc=kc, Fp=Fp, blk=blk, m=m, c0=c0, hT=hT: e.matmul(
                            Fp[0:m, (blk % 4) * 128:(blk % 4 + 1) * 128], lhsT=win[:, kc, c0:c0 + m], rhs=hT[:, kc, :],
                            start=(kc == 0), stop=(kc == 7)), reads=[r_hT, r_win], writes=[rF], flag=(kc == 7))
                zf, r_zf, d_zf = B["zf"].next()
                F0, F1, F2 = Fb[0][0], Fb[1][0], Fb[2][0]
                rF0, rF1, rF2 = Fb[0][1], Fb[1][1], Fb[2][1]
                self.op(ACT, lambda e, zf=zf: e.activation(out=zf[:, 0, :], in_=F0[:, 0:128], func=AF.Copy,
                                                           scale=32.0 ** -0.5), reads=[rF0], writes=[r_zf])
                self.op(DVE, lambda e, zf=zf: e.tensor_copy(out=zf[:, 1, :], in_=F0[:, 128:256]), reads=[rF0],
                        writes=[r_zf])
                self.op(ACT, lambda e, zf=zf: e.activation(out=zf[:, 2:4, :].rearrange("p a b -> p (a b)"),
                                                           in_=F0[:, 256:512], func=AF.Silu), reads=[rF0], writes=[r_zf])
                self.op(ACT, lambda e, zf=zf: e.activation(out=zf[:, 4:6, :].rearrange("p a b -> p (a b)"),
                                                           in_=F1[:, 0:256], func=AF.Gelu_apprx_tanh), reads=[rF1],
                        writes=[r_zf])
                sg, r_sg, _ = B["sg"].next()
                self.op(ACT, lambda e, sg=sg: e.activation(out=sg[:].rearrange("p a b -> p (a b)"), in_=F2[:, 0:256],
                                                           func=AF.Sigmoid), reads=[rF2], writes=[r_sg])
                self.op(DVE, lambda e, sg=sg, zf=zf: e.tensor_tensor(out=zf[:, 6:8, :].rearrange("p a b -> p (a b)"),
                                                                     in0=F1[:, 256:512],
                                                                     in1=sg[:].rearrange("p a b -> p (a b)"), op=ALU.mult),
                        reads=[rF1, r_sg], writes=[r_zf])
                self.op(DVE, lambda e, zf=zf: e.tensor_copy(out=zf[0:32, 8, :], in_=F2[0:32, 256:384]), reads=[rF2],
                        writes=[r_zf])
                self.op(SP, lambda e, zf=zf, i=i: e.dma_start(
                    out=self.ZF[sq][:, :, i * 128:(i + 1) * 128].rearrange("b p t -> p b t"), in_=zf[:]),
                    reads=[r_zf], writes=[self.r_dram["ZF" + sq]], dsem=d_zf)
                self.chk("a3")
                yield
                (M0, r_M0), (M1, r_M1) = B["M0"], B["M1"]
                self.op(PE, lambda e, zf=zf: e.matmul(M0[:, 0:256], lhsT=zf[0:33, 8, :], rhs=W["wg"][0:33, :], start=True,
                                                      stop=True), reads=[r_zf, W["r_wg"]], writes=[r_M0])
                ee, r_ee, _ = B["ee"].next()
                self.op(ACT, lambda e, ee=ee: e.activation(out=ee[:], in_=M0[:, 0:256], func=AF.Exp, scale=-1.0),
                        reads=[r_M0], writes=[r_ee])
                self.op(ACT, lambda e, ee=ee: e.activation(out=ee[:], in_=ee[:], func=AF.Ln, bias=1.0, scale=1.0),
                        reads=[r_ee], writes=[r_ee])
                g, r_g, d_g = B["g"].next()
                self.op(DVE, lambda e, ee=ee, g=g: e.tensor_scalar(out=g[:], in0=ee[:], scalar1=-1.0 / 16.0, scalar2=-1.0,
                                                                   op0=ALU.mult, op1=ALU.max), reads=[r_ee], writes=[r_g])
                self.op(SP, lambda e, g=g, i=i: e.dma_start(out=self.G[sq][i * 128:(i + 1) * 128, :], in_=g[:]),
                        reads=[r_g], writes=[self.r_dram["G" + sq]], dsem=d_g)
                self.chk("a4")
                yield
                for dr in range(2):
                    self.op(PE, lambda e, dr=dr, g=g: e.matmul(M0[:, 256 + dr * 128:384 + dr * 128],
                                                               lhsT=self.mrev[:, dr, :], rhs=g[:, dr * 128:(dr + 1) * 128],
                                                               start=True, stop=True), reads=[r_g, self.r_mrev],
                            writes=[r_M0])
                    self.op(PE, lambda e, dr=dr, g=g: e.matmul(F2[:, 384 + 64 * dr:448 + 64 * dr],
                                                               lhsT=g[:, dr * 128:(dr + 1) * 128], rhs=self.ind[:],
                                                               start=True, stop=True), reads=[r_g, self.r_ind],
                            writes=[rF2])
                e3, r_e3, _ = B["e3"].next()
                self.op(ACT, lambda e, e3=e3: e.activation(out=e3[:], in_=M0[:, 256:512], func=AF.Exp), reads=[r_M0],
                        writes=[r_e3])
                dec, r_dec, _ = B["dec"].next()
                for dr in range(2):
                    self.op(ACT, lambda e, dec=dec, dr=dr: e.activation(out=dec[:, 2 * dr:2 * dr + 2],
                                                                        in_=F2[:, 384 + 64 * dr:386 + 64 * dr],
                                                                        func=AF.Exp), reads=[rF2], writes=[r_dec])
                self.chk("a41")
                yield
                kdec, r_kdec, _ = B["kdec"].next()
                for dr in range(2):
                    for j in range(2):
                        self.op(DVE, lambda e, dr=dr, j=j, kdec=kdec, e3=e3, zt=zt: e.scalar_tensor_tensor(
                            out=kdec[:, dr, j, :], in0=zt[:, 0:128], scalar=self.ind[:, j:j + 1],
                            in1=e3[:, dr * 128:(dr + 1) * 128], op0=ALU.mult, op1=ALU.mult),
                            reads=[r_zt, r_e3, self.r_ind], writes=[r_kdec])
                self.chk("a42")
                yield
                kvc, r_kvc, _ = B["kvc"].next()
                for dr in range(2):
                    for j in range(2):
                        self.op(PE, lambda e, dr=dr, j=j, kdec=kdec, zt=zt: e.matmul(
                            M1[:, j * 256:(j + 1) * 256], lhsT=kdec[:, dr, j, :],
                            rhs=zt[:, 128:384], start=True, stop=True), reads=[r_kdec, r_zt],
                            writes=[r_M1])
                    self.chk("a43")
                    kvm, r_kvm, _ = B["kvm"].next()
                    self.op(DVE, lambda e, kvm=kvm: e.tensor_tensor(out=kvm[:], in0=M1[:, :], in1=self.bm[:], op=ALU.mult),
                            reads=[r_M1, self.r_bm], writes=[r_kvm])
                    kv4 = kvm[:].rearrange("p (j h e) -> p j h e", j=2, h=4)
                    self.op(DVE, lambda e, kv4=kv4: e.tensor_tensor(out=kv4[:, :, 0, :], in0=kv4[:, :, 0, :],
                                                                    in1=kv4[:, :, 1, :], op=ALU.add),
                            reads=[r_kvm], writes=[r_kvm])
                    self.op(DVE, lambda e, kv4=kv4: e.tensor_tensor(out=kv4[:, :, 2, :], in0=kv4[:, :, 2, :],
                                                                    in1=kv4[:, :, 3, :], op=ALU.add),
                            reads=[r_kvm], writes=[r_kvm])
                    self.op(DVE, lambda e, kv4=kv4, kvc=kvc, dr=dr: e.tensor_tensor(
                        out=kvc[:, dr, :, :], in0=kv4[:, :, 0, :], in1=kv4[:, :, 2, :], op=ALU.add),
                        reads=[r_kvm], writes=[r_kvc])
                self.chk("a5")
                yield
                for j in range(2):
                    n = self.choff[sq] + 2 * i + j
                    cur = self.chain_i % 2
                    self.op(POOL, lambda e, n=n, cur=cur: e.tensor_copy(out=self.SF[:, n, :], in_=self.stf[:, cur, :]),
                            reads=[self.r_stf], writes=[self.r_SF])
                    self.op(DVE, lambda e, j=j, cur=cur, dec=dec, kvc=kvc: e.scalar_tensor_tensor(
                        out=self.stf[:, 1 - cur, :], in0=self.stf[:, cur, :], scalar=dec[:, j:j + 1], in1=kvc[:, 0, j, :],
                        op0=ALU.mult, op1=ALU.add), reads=[self.r_stf, r_dec, r_kvc], writes=[self.r_stf])
                    self.chain_i += 1
                    self.op(POOL, lambda e, n=n, j=j, kvc=kvc: e.tensor_copy(out=self.KVB[:, n, :], in_=kvc[:, 1, j, :]),
                            reads=[r_kvc], writes=[self.r_KVB])
                    self.op(POOL, lambda e, n=n, j=j, dec=dec: e.tensor_copy(out=self.DECB[:, n:n + 1],
                                                                             in_=dec[:, 2 + j:3 + j]),
                            reads=[r_dec], writes=[self.r_KVB])

        self.pipeline(nt, ldx, tileA, depth=PIPE, lag=LAG_A)

    def bscan(self, pre):
        order = list(range(self.nch["ctx"] - 1, -1, -1)) + \
            [self.choff["lat"] + n for n in range(self.nch["lat"] - 1, -1, -1)]
        self.op(POOL, lambda e: e.memset(self.stf[:, 0, :], 0.0), writes=[self.r_stf])
        cur = 0
        for n in order:
            self.op(POOL, lambda e, n=n, cur=cur: e.tensor_copy(out=self.SB_[:, n, :], in_=self.stf[:, cur, :]),
                    reads=[self.r_stf], writes=[self.r_SB])
            self.op(DVE, lambda e, n=n, cur=cur: e.scalar_tensor_tensor(
                out=self.stf[:, 1 - cur, :], in0=self.stf[:, cur, :], scalar=self.DECB[:, n:n + 1],
                in1=self.KVB[:, n, :], op0=ALU.mult, op1=ALU.add), reads=[self.r_stf, self.r_KVB],
                writes=[self.r_stf])
            cur = 1 - cur

    def bufsB(self, pre):
        B = {}
        p2 = pre + "B_"
        B["zt"] = Ring(self, p2 + "zt", [128, NZT], BF16, 4, dma=True)
        B["zf"] = Ring(self, p2 + "zf", [128, 9, 128], BF16, 2, dma=True)
        B["yw"] = Ring(self, p2 + "yw", [128, 2, 160], BF16, 2, dma=True)
        B["g"] = Ring(self, p2 + "g", [128, 256], F32, 2, dma=True)
        B["xt"] = Ring(self, p2 + "xt", [128, D], F32, 2, dma=True)
        B["e1"] = Ring(self, p2 + "e1", [128, 2, 128], F32, 2)
        B["e2"] = Ring(self, p2 + "e2", [128, 2, 128], F32, 2)
        B["kin"] = Ring(self, p2 + "kin", [128, 2, 128], BF16, 2)
        B["qm"] = Ring(self, p2 + "qm", [128, 2, 4, 128], BF16, 2)
        B["atm"] = Ring(self, p2 + "atm", [128, 2, 512], BF16, 2)
        B["sqo"] = Ring(self, p2 + "sqo", [128, 256], F32, 2)
        B["ssq"] = Ring(self, p2 + "ssq", [128, 8], F32, 2)
        B["on"] = Ring(self, p2 + "on", [128, 256], BF16, 2)
        B["yT"] = Ring(self, p2 + "yT", [128, 8, 128], BF16, 2)
        B["pp"] = Ring(self, p2 + "pp", [128, 2, 128], BF16, 2)
        B["cn"] = Ring(self, p2 + "cn", [128, 256], BF16, 2)
        B["t1"] = Ring(self, p2 + "t1", [128, D], F32, 2)
        B["x1"] = Ring(self, p2 + "x1", [128, D], F32, 2, dma=True)
        B["hn"] = Ring(self, p2 + "hn", [128, D], BF16, 2)
        B["h2"] = Ring(self, p2 + "h2", [128, 8, 128], BF16, 2, dma=True)
        for nm in ("b0", "b1", "b2", "b3", "b4", "b6", "b7"):
            B[nm] = self.ps(p2 + nm)
        B["b5"] = self.ps(p2 + "b5", BF16)
        for nm in ("cum", "AT0", "AT1", "pl", "sg", "tr", "Y0", "Y1"):
            B["r_" + nm] = Res()
        B["r_O"] = B["r_cum"]
        B["r_po"] = B["r_pl"]
        B["r_cv"] = B["r_Y0"]
        return B

    def passB(self, B, W, sq, xsrc, Ls):
        si = 0 if sq == "lat" else 1
        nt = Ls // 128
        b0, b1, b2, b3, b4, b5, b6, b7 = (B["b%d" % k] for k in range(8))
        AT = [b1, b2]
        r_AT = [B["r_AT0"], B["r_AT1"]]
        mT, r_mT = self.mT, self.r_mT
        fvec, r_fvec = W["fvec"], W["r_fvec"]
        ZT, ZF, G = self.ZT[sq], self.ZF[sq], self.G[sq]
        rZT, rZF, rG = self.r_dram["ZT" + sq], self.r_dram["ZF" + sq], self.r_dram["G" + sq]
        zts = {}
        lds = {}

        def load_zt(i):
            t, r, d = B["zt"].next()
            self.op(SP, lambda e: e.dma_start(out=t[:], in_=ZT[i * 128:(i + 1) * 128, :]), reads=[rZT], writes=[r],
                    dsem=d)
            zts[i] = (t, r)

        def load_rest(i):
            zf, r_zf, d_zf = B["zf"].next()
            self.op(SP, lambda e: e.dma_start(
                out=zf[:], in_=ZF[:, :, i * 128:(i + 1) * 128].rearrange("b p t -> p b t")), reads=[rZF],
                writes=[r_zf], dsem=d_zf)
            yw, r_yw, d_yw = B["yw"].next()
            lo, hi = max(i * 128 - 16, 0), min(i * 128 + 144, Ls)
            if lo > i * 128 - 16 or hi < i * 128 + 144:
                self.op(POOL, lambda e: e.memset(yw[:], 0.0), writes=[r_yw])
            o0 = lo - (i * 128 - 16)
            self.op(SP, lambda e: e.dma_start(
                out=yw[:, :, o0:o0 + hi - lo], in_=ZF[6:8, :, lo:hi].rearrange("b p t -> p b t")), reads=[rZF],
                writes=[r_yw], dsem=d_yw)
            g, r_g, d_g = B["g"].next()
            self.op(SP, lambda e: e.dma_start(out=g[:], in_=G[i * 128:(i + 1) * 128, :]), reads=[rG],
                    writes=[r_g], dsem=d_g)
            xt, r_xt, d_xt = B["xt"].next()
            self.load(SP, xt[:], xsrc[i * 128:(i + 1) * 128, :], r_xt, d_xt)
            lds[i] = (zf, r_zf, yw, r_yw, g, r_g, xt, r_xt)

        load_zt(0)

        def prefB(n):
            if n + 1 < nt:
                load_zt(n + 1)
            load_rest(n)

        def tileB(i):
                zt, r_zt = zts[i]
                zf, r_zf, yw, r_yw, g, r_g, xt, r_xt = lds.pop(i)
                yT, r_yT, _ = B["yT"].next()
                for dr in range(2):
                    self.op(PE, lambda e, dr=dr, g=g: e.matmul(b0[:, dr * 128:(dr + 1) * 128],
                                                               lhsT=g[:, dr * 128:(dr + 1) * 128],
                                                               rhs=self.mcum[:, dr, 0:128], start=True, stop=True),
                            reads=[r_g, self.r_mcum], writes=[B["r_cum"]])
                e1, r_e1, _ = B["e1"].next()
                e2, r_e2, _ = B["e2"].next()
                self.op(ACT, lambda e, e1=e1: e.activation(out=e1[:].rearrange("p a b -> p (a b)"), in_=b0[:, 0:256],
                                                           func=AF.Exp), reads=[B["r_cum"]], writes=[r_e1])
                self.op(ACT, lambda e, e2=e2: e.activation(out=e2[:].rearrange("p a b -> p (a b)"), in_=b0[:, 0:256],
                                                           func=AF.Exp, scale=-1.0), reads=[B["r_cum"]], writes=[r_e2])
                kin, r_kin, _ = B["kin"].next()
                qm, r_qm, _ = B["qm"].next()
                for dr in range(2):
                    self.op(DVE, lambda e, dr=dr, kin=kin, zf=zf, e2=e2: e.tensor_tensor(
                        out=kin[:, dr, :], in0=zf[:, 1, :], in1=e2[:, dr, :], op=ALU.mult), reads=[r_zf, r_e2],
                        writes=[r_kin])
                    for h in range(4):
                        eng = DVE
                        self.op(eng, lambda e, dr=dr, h=h, qm=qm, zf=zf, e1=e1: e.scalar_tensor_tensor(
                            out=qm[:, dr, h, :], in0=zf[:, 0, :], scalar=self.hm[:, h:h + 1], in1=e1[:, dr, :],
                            op0=ALU.mult, op1=ALU.mult), reads=[r_zf, r_e1, self.r_hm], writes=[r_qm])
                self.chk("b1")
                yield
                atm, r_atm, _ = B["atm"].next()
                for dr in range(2):
                    for h in range(4):
                        self.op(PE, lambda e, dr=dr, h=h, kin=kin, qm=qm: e.matmul(
                            AT[dr][:, h * 128:(h + 1) * 128], lhsT=kin[:, dr, :], rhs=qm[:, dr, h, :], start=True,
                            stop=True), reads=[r_kin, r_qm], writes=[r_AT[dr]], flag=(h == 3))
                    self.op(DVE, lambda e, dr=dr, atm=atm: e.tensor_tensor(out=atm[:, dr, :], in0=AT[dr][:, :],
                                                                           in1=self.mcum[:, dr, :], op=ALU.mult),
                            reads=[r_AT[dr], self.r_mcum], writes=[r_atm])
                self.chk("b2")
                yield
                SS = [(self.SF, self.r_SF), (self.SB_, self.r_SB)]
                for h in range(4):
                    for dr in range(2):
                        self.op(PE, lambda e, dr=dr, h=h, atm=atm, zt=zt: e.matmul(
                            b0[:, 256 + 64 * h:320 + 64 * h], lhsT=atm[:, dr, h * 128:(h + 1) * 128],
                            rhs=zt[:, 128 + 64 * h:192 + 64 * h], start=(dr == 0), stop=False, skip_group_check=True),
                            reads=[r_atm, r_zt], writes=[B["r_O"]], flag=False)
                        for j in range(2):
                            n = self.choff[sq] + 2 * i + j
                            lastmm = (dr == 1 and j == 1)
                            self.op(PE, lambda e, dr=dr, h=h, j=j, n=n, qm=qm, lastmm=lastmm: e.matmul(
                                b0[64 * j:64 * j + 64, 256 + 64 * h:320 + 64 * h], lhsT=qm[:, dr, h, 64 * j:64 * j + 64],
                                rhs=SS[dr][0][:, n, :], start=False, stop=lastmm, skip_group_check=True),
                                reads=[r_qm, SS[dr][1]], writes=[B["r_O"]], flag=(lastmm and h == 3))
                self.chk("b3")
                sqo, r_sqo, _ = B["sqo"].next()
                ssq, r_ssq, _ = B["ssq"].next()
                self.op(ACT, lambda e, sqo=sqo: e.activation(out=sqo[:], in_=b0[:, 256:512], func=AF.Square),
                        reads=[B["r_O"]], writes=[r_sqo])
                self.op(DVE, lambda e, sqo=sqo, ssq=ssq: e.tensor_reduce(
                    out=ssq[:, 0:4], in_=sqo[:].rearrange("p (h e) -> p h e", h=4), axis=AX.X, op=ALU.add),
                    reads=[r_sqo], writes=[r_ssq])
                self.op(ACT, lambda e, ssq=ssq: e.activation(out=ssq[:, 0:4], in_=ssq[:, 0:4], func=AF.Sqrt,
                                                             scale=1.0 / 64.0, bias=EPS), reads=[r_ssq], writes=[r_ssq])
                self.op(DVE, lambda e, ssq=ssq: e.reciprocal(out=ssq[:, 4:8], in_=ssq[:, 0:4]), reads=[r_ssq],
                        writes=[r_ssq])
                on, r_on, _ = B["on"].next()
                for h in range(4):
                    self.op(DVE, lambda e, h=h, on=on, ssq=ssq: e.tensor_scalar(
                        out=on[:, 64 * h:64 * h + 64], in0=b0[:, 256 + 64 * h:320 + 64 * h], scalar1=ssq[:, 4 + h:5 + h],
                        scalar2=None, op0=ALU.mult), reads=[B["r_O"], r_ssq], writes=[r_on])
                for blk in range(2):
                    self.op(PE, lambda e, blk=blk, on=on: e.transpose(b5[:, blk * 128:(blk + 1) * 128],
                                                                      on[:, blk * 128:(blk + 1) * 128], self.identb[:]),
                            reads=[r_on, self.r_identb], writes=[B["r_tr"]], flag=(blk == 1))
                for blk in range(2):
                    self.op(DVE, lambda e, blk=blk, yT=yT, zf=zf: e.scalar_tensor_tensor(
                        out=yT[:, blk, :], in0=b5[:, blk * 128:(blk + 1) * 128], scalar=fvec[:, blk:blk + 1],
                        in1=zf[:, 2 + blk, :], op0=ALU.mult, op1=ALU.mult), reads=[B["r_tr"], r_fvec, r_zf],
                        writes=[r_yT])
                self.chk("b4")
                yield
                var = 0 if i == 0 else (2 if i == nt - 1 else 1)
                if nt == 1:
                    var = 0
                for gi in range(4):
                    po_ = b3[64 * (gi % 2):64 * (gi % 2) + 64, (gi // 2) * 128:(gi // 2 + 1) * 128]
                    parts = [(zt, r_zt, 0, 128, self.pbm[:, var, gi, :], self.r_pbm)]
                    if i > 0:
                        parts.append((zts[i - 1][0], zts[i - 1][1], 0, 128, self.pbp[:, gi, :], self.r_pbp))
                    if i + 1 < nt:
                        parts.append((zts[i + 1][0], zts[i + 1][1], 0, 128, self.pbn[:, gi, :], self.r_pbn))
                    for k_, (zz, rzz, p0, p1, rhs, rr) in enumerate(parts):
                        self.op(PE, lambda e, po_=po_, zz=zz, p0=p0, p1=p1, rhs=rhs, gi=gi, k_=k_, np_=len(parts): e.matmul(
                            po_, lhsT=zz[p0:p1, 384 + 64 * gi:448 + 64 * gi], rhs=rhs, start=(k_ == 0),
                            stop=(k_ == np_ - 1)), reads=[rzz, rr], writes=[B["r_pl"]],
                            flag=(gi == 3 and k_ == len(parts) - 1))
                pp, r_pp, _ = B["pp"].next()
                self.op(ACT, lambda e, pp=pp: e.activation(out=pp[:].rearrange("p a b -> p (a b)"), in_=b3[:, 0:256],
                                                           func=AF.Copy), reads=[B["r_pl"]], writes=[r_pp])
                for gp in range(2):
                    self.op(PE, lambda e, gp=gp, pp=pp: e.matmul(b3[:, 256 + gp * 128:384 + gp * 128], lhsT=W["pw"][:, gp, :],
                                                                 rhs=pp[:, gp, :], start=True, stop=True),
                            reads=[r_pp, W["r_pw"]], writes=[B["r_po"]], flag=(gp == 1))
                for gp in range(2):
                    self.op(ACT, lambda e, gp=gp, yT=yT: e.activation(out=yT[:, 2 + gp, :],
                                                                     in_=b3[:, 256 + gp * 128:384 + gp * 128],
                                                                     func=AF.Copy, scale=fvec[:, 2 + gp:3 + gp]),
                            reads=[B["r_po"], r_fvec], writes=[r_yT])
                self.chk("b5")
                yield
                for h in range(4):
                    so_ = b4[64 * (h % 2):64 * (h % 2) + 64, (h // 2) * 128:(h // 2 + 1) * 128]
                    import os
                    NOSB = os.environ.get("SGUBIAS", "1") == "0"
                    self.op(PE, lambda e, so_=so_, h=h, zt=zt: e.matmul(so_, lhsT=zt[:, 640 + 64 * h:704 + 64 * h],
                                                                        rhs=W["sw"][:, h, :], start=True, stop=NOSB),
                            reads=[r_zt, W["r_sw"]], writes=[B["r_sg"]], flag=(NOSB and h == 3))
                    if NOSB:
                        continue
                    self.op(PE, lambda e, so_=so_, h=h: e.matmul(so_, lhsT=self.onesf[0:1, 0:64],
                                                                 rhs=W["sgub"][0:1, h * 128:(h + 1) * 128], start=False,
                                                                 stop=True), reads=[self.r_onesf, W["r_sgub"]],
                            writes=[B["r_sg"]], flag=(h == 3))
                for hp in range(2):
                    self.op(DVE, lambda e, hp=hp, yT=yT, zf=zf: e.tensor_tensor(
                        out=yT[:, 4 + hp, :], in0=b4[:, hp * 128:(hp + 1) * 128], in1=zf[:, 4 + hp, :], op=ALU.mult),
                        reads=[B["r_sg"], r_zf], writes=[r_yT])
                self.chk("b6")
                yield
                import os
                CV = os.environ.get("CONVVAR", "")
                for blk in range(2):
                    for j in range({"few": 8, "one": 1, "none": 0, "skip": 0}.get(CV, 31)):
                        self.op(PE, lambda e, blk=blk, j=j, yw=yw: e.matmul(
                            b6[:, blk * 128:(blk + 1) * 128], lhsT=yw[:, blk, 1 + j:129 + j],
                            rhs=W["d31"][:, blk, j, :], start=(j == 0), stop=False), reads=[r_yw, W["r_d31"]],
                            writes=[B["r_Y0"]], flag=False)
                    if CV in ("nobias", "one", "skip"):
                        continue
                    self.op(PE, lambda e, blk=blk: e.matmul(b6[:, blk * 128:(blk + 1) * 128],
                                                            lhsT=self.onesf[0:1, 0:128],
                                                            rhs=W["convb"][0:1, blk * 128:(blk + 1) * 128], start=False,
                                                            stop=True), reads=[self.r_onesf, W["r_convb"]],
                            writes=[B["r_Y0"]], flag=(blk == 1))
                self.chk("b61")
                rs_, nb_, r_mvc = self.ln_stats(b6[:, 0:256], 256, B["r_Y0"])
                cn, r_cn, _ = B["cn"].next()
                self.op(ACT, lambda e, cn=cn, rs_=rs_, nb_=nb_: e.activation(out=cn[:], in_=b6[:, 0:256],
                                                                            func=AF.Identity, scale=rs_, bias=nb_),
                        reads=[B["r_Y0"], r_mvc], writes=[r_cn])
                self.chk("b62")
                yield
                for blk in range(2):
                    self.op(PE, lambda e, blk=blk, cn=cn: e.transpose(b5[:, 256 + blk * 128:384 + blk * 128],
                                                                      cn[:, blk * 128:(blk + 1) * 128], self.identb[:]),
                            reads=[r_cn, self.r_identb], writes=[B["r_tr"]], flag=(blk == 1))
                for blk in range(2):
                    self.op(ACT, lambda e, blk=blk, yT=yT: e.activation(
                        out=yT[:, 6 + blk, :], in_=b5[:, 256 + blk * 128:384 + blk * 128], func=AF.Silu,
                        scale=fvec[:, 4 + blk:5 + blk], bias=fvec[:, 6 + blk:7 + blk]), reads=[B["r_tr"], r_fvec],
                        writes=[r_yT])
                self.chk("b7")
                yield
                if "YD" in DEBUG and sq == "lat":
                    self.op(SP, lambda e, yT=yT, i=i: e.dma_start(
                        out=self.YD[:, :, i * 128:(i + 1) * 128].rearrange("k p t -> p k t"), in_=yT[:]),
                        reads=[r_yT], writes=[self.r_YD], dsem=self.ydsem)
                Y = [b6, b7]
                r_Y = [B["r_Y0"], B["r_Y1"]]
                for nh in range(2):
                    for kc in range(8):
                        self.op(PE, lambda e, nh=nh, kc=kc, yT=yT: e.matmul(
                            Y[nh][:, :], lhsT=yT[:, kc, :], rhs=W["wout"][:, kc, nh * 512:(nh + 1) * 512],
                            start=(kc == 0), stop=(kc == 7)), reads=[r_yT, W["r_wout"]], writes=[r_Y[nh]],
                            flag=(kc == 7))
                self.chk("b8")
                t1, r_t1, _ = B["t1"].next()
                for nh in range(2):
                    self.op(DVE, lambda e, nh=nh, t1=t1: e.tensor_tensor(
                        out=t1[:, nh * 512:(nh + 1) * 512], in0=Y[nh][:, :], in1=self.grow[:, si, 0, nh * 512:(nh + 1) * 512],
                        op=ALU.mult), reads=[r_Y[nh], self.r_grow], writes=[r_t1])
                self.op(DVE, lambda e, t1=t1, xt=xt: e.scalar_tensor_tensor(out=t1[:], in0=xt[:], scalar=ALPHA, in1=t1[:],
                                                                             op0=ALU.mult, op1=ALU.add),
                        reads=[r_xt, r_t1], writes=[r_t1])
                self.chk("b9")
                yield
                yield from self.residual_tail(B, W, t1, r_t1, 0, si, sq, i, mod=True)
                self.chk("b10")
                self.chk("T:%s:%d" % (sq, i))

        self.pipeline(nt, prefB, tileB, depth=PIPE, lag=LAG_B)

    def residual_tail(self, B, W, t1, r_t1, which, si, sq, i, mod, dst=None, rdst=None):
        prow, r_prow = W["prow"], W["r_prow"]
        rs_, nb_, r_mv = self.ln_stats(t1, D, r_t1)
        x1, r_x1, d_x1 = B["x1"].next()
        self.op(ACT, lambda e: e.activation(out=x1[:], in_=t1[:], func=AF.Identity, scale=rs_, bias=nb_),
                reads=[r_t1, r_mv], writes=[r_x1])
        self.op(POOL, lambda e: e.tensor_tensor(out=x1[:], in0=x1[:], in1=prow[:, 2 * which, :], op=ALU.mult),
                reads=[r_x1, r_prow], writes=[r_x1])
        self.op(DVE, lambda e: e.tensor_tensor(out=x1[:], in0=x1[:], in1=prow[:, 2 * which + 1, :], op=ALU.add),
                reads=[r_x1, r_prow], writes=[r_x1])
        if dst is None:
            dst, rdst = self.X1[sq], self.r_dram["X1" + sq]
        self.op(SP, lambda e: e.dma_start(out=dst[i * 128:(i + 1) * 128, :], in_=x1[:]), reads=[r_x1],
                writes=[rdst], dsem=d_x1)
        if not mod:
            return
        yield
        rs2, nb2, r_mv2 = self.ln_stats(x1, D, r_x1)
        hn, r_hn, _ = B["hn"].next()
        self.op(ACT, lambda e: e.activation(out=hn[:], in_=x1[:], func=AF.Identity, scale=rs2, bias=nb2),
                reads=[r_x1, r_mv2], writes=[r_hn])
        yield
        b5 = B["b5"]
        for kc in range(8):
            self.op(PE, lambda e, kc=kc: e.transpose(b5[:, kc * 128:(kc + 1) * 128], hn[:, kc * 128:(kc + 1) * 128],
                                                     self.identb[:]), reads=[r_hn, self.r_identb],
                    writes=[B["r_tr"]], flag=(kc == 7))
        h2, r_h2, d_h2 = B["h2"].next()
        for kc in range(8):
            if kc % 2 == 0:
                self.op(ACT, lambda e, kc=kc: e.activation(
                    out=h2[:, kc, :], in_=b5[:, kc * 128:(kc + 1) * 128], func=AF.Identity,
                    scale=self.mT[:, si, 4, kc:kc + 1], bias=self.mT[:, si, 3, kc:kc + 1]),
                    reads=[B["r_tr"], self.r_mT], writes=[r_h2])
            else:
                self.op(DVE, lambda e, kc=kc: e.tensor_scalar(
                    out=h2[:, kc, :], in0=b5[:, kc * 128:(kc + 1) * 128], scalar1=self.mT[:, si, 4, kc:kc + 1],
                    scalar2=self.mT[:, si, 3, kc:kc + 1], op0=ALU.mult, op1=ALU.add),
                    reads=[B["r_tr"], self.r_mT], writes=[r_h2])
        self.op(SP, lambda e: e.dma_start(out=self.H2T[sq][:, :, i * 128:(i + 1) * 128].rearrange("k p t -> p k t"),
                                          in_=h2[:]), reads=[r_h2], writes=[self.r_dram["H2" + sq]], dsem=d_h2)

    def bufsC(self, pre):
        B = {}
        p2 = pre + "C_"
        RL, WL = self.R, GW
        wmax = max((RL + 2) * WL, LC)
        B["hw"] = Ring(self, p2 + "hw", [128, 8, wmax], BF16, 2, dma=True)
        B["wu"] = Ring(self, p2 + "wu", [128, 8, 256], BF16, 3, dma=True)
        B["dg"] = Ring(self, p2 + "dg", [128, 2, 9, 128], BF16, 2)
        B["U"] = Ring(self, p2 + "U", [128, max((RL + 2) * (WL + 2), LC + 2)], BF16, 4)
        B["sgl"] = Ring(self, p2 + "sgl", [128, max(RL * WL, LC)], F32, 2)
        B["act"] = Ring(self, p2 + "act", [128, NPAIR, max(RL * WL, LC)], BF16, 1)
        B["x1t"] = Ring(self, p2 + "x1t", [128, D], F32, 2, dma=True)
        B["t1"] = Ring(self, p2 + "t1", [128, D], F32, 2)
        B["x1"] = Ring(self, p2 + "x2", [128, D], F32, 2, dma=True)
        B["up"] = Ring(self, p2 + "up", [128, 512], F32, 4, psum=True)
        B["cv"] = Ring(self, p2 + "cv", [128, 512], F32, 2, psum=True)
        B["F"] = Ring(self, p2 + "F", [128, 512], F32, 2, psum=True)
        return B

    def passC(self, B, W, sq, xdst, rows, Wd, R):
        si = 0 if sq == "lat" else 1
        for t_, r_ in zip(B["U"].t, B["U"].r):
            self.op(POOL, lambda e, t_=t_: e.memset(t_[:], 0.0), writes=[r_])
        Ts = R * Wd
        nst = rows // R
        H2, rH2 = self.H2T[sq], self.r_dram["H2" + sq]
        X1, rX1 = self.X1[sq], self.r_dram["X1" + sq]
        cw9, r_cw9 = W["cw9"], W["r_cw9"]
        drs = [0] if rows == 1 else [-1, 0, 1]
        Wp = Wd + 2
        hws = {}

        def load_hw(s):
            r0 = s * R
            lo, hi = max(r0 - 1, 0), min(r0 + R + 1, rows)
            hw, r_hw, d_hw = B["hw"].next()
            self.op(SP, lambda e: e.dma_start(
                out=hw[:, :, 0:(hi - lo) * Wd], in_=H2[:, :, lo * Wd:hi * Wd].rearrange("k p t -> p k t")),
                reads=[rH2], writes=[r_hw], dsem=d_hw)
            hws[s] = (hw, r_hw)

        load_hw(0)
        for s in range(nst):
            r0 = s * R
            lo, hi = max(r0 - 1, 0), min(r0 + R + 1, rows)
            nrow = hi - lo
            boff = lo - (r0 - 1)
            hw, r_hw = hws.pop(s)
            act, r_act, _ = B["act"].next()
            rpg = max(1, 512 // Wd)
            groups = []
            a = 0
            ng = (nrow + rpg - 1) // rpg
            per = (nrow + ng - 1) // ng
            while a < nrow:
                groups.append((a, min(a + per, nrow)))
                a += per
            dgs = {}

            def build_dg(jj):
                dg, r_dg, _ = B["dg"].next()
                for half in range(2):
                    blk = jj + NPAIR * half
                    for dr in drs:
                        for dc in (-1, 0, 1):
                            tap = (dr + 1) * 3 + (dc + 1)
                            self.op(DVE, lambda e, dg=dg, half=half, tap=tap, blk=blk: e.tensor_scalar(
                                out=dg[:, half, tap, :], in0=self.identb[:], scalar1=cw9[:, blk, tap:tap + 1],
                                scalar2=None, op0=ALU.mult), reads=[self.r_identb, r_cw9], writes=[r_dg])
                return dg, r_dg

            for j in range(NPAIR):
                wu, r_wu, d_wu = B["wu"].next()
                self.op(SP, lambda e, wu=wu, j=j: e.dma_start(out=wu[:].rearrange("p a b -> p (a b)"),
                                                              in_=self.WUPB[j]), reads=[self.r_dram["WUPB"]],
                        writes=[r_wu], dsem=d_wu)
                if j == 0:
                    dgs[0] = build_dg(0)
                dg, r_dg = dgs.pop(j)
                cvs = []
                Us = []
                for half in range(2):
                    U, r_U, _ = B["U"].next()
                    Uv = U[:, 0:(R + 2) * Wp].rearrange("p (r w) -> p r w", w=Wp) if rows > 1 else None
                    Us.append((U, r_U, Uv))
                    if rows > 1 and r0 == 0:
                        self.op(POOL, lambda e, Uv=Uv: e.memset(Uv[:, 0, :], 0.0), writes=[r_U])
                    if rows > 1 and hi == rows and r0 + R + 1 > rows:
                        self.op(POOL, lambda e, Uv=Uv: e.memset(Uv[:, R + 1, :], 0.0), writes=[r_U])
                    for gi, (ga, gb) in enumerate(groups):
                        up, r_up, _ = B["up"].next()
                        ntok = (gb - ga) * Wd
                        for kc in range(8):
                            self.op(PE, lambda e, up=up, kc=kc, half=half, ga=ga, ntok=ntok, wu=wu, hw=hw: e.matmul(
                                up[:, 0:ntok], lhsT=wu[:, kc, half * 128:(half + 1) * 128],
                                rhs=hw[:, kc, ga * Wd:ga * Wd + ntok], start=(kc == 0), stop=(kc == 7)),
                                reads=[r_wu, r_hw], writes=[r_up], flag=(kc == 7))
                        b_a = boff + ga if rows > 1 else 1
                        dstv = Uv[:, b_a:b_a + (gb - ga), 1:1 + Wd] if rows > 1 else U[:, 1:1 + Wd]
                        srcv = up[:, 0:ntok].rearrange("p (r w) -> p r w", w=Wd) if rows > 1 else up[:, 0:ntok]
                        self.op(ACT, lambda e, dstv=dstv, srcv=srcv: e.activation(out=dstv, in_=srcv, func=AF.Copy),
                                reads=[r_up], writes=[r_U])
                if j + 1 < NPAIR:
                    dgs[j + 1] = build_dg(j + 1)
                for half in range(2):
                    U, r_U, Uv = Us[half]
                    cv, r_cv, _ = B["cv"].next()
                    taps = [(dr, dc) for dr in drs for dc in (-1, 0, 1)]
                    for ti, (dr, dc) in enumerate(taps):
                        tap = (dr + 1) * 3 + (dc + 1)
                        if rows > 1:
                            rhs = Uv[:, 1 + dr:1 + dr + R, 1 + dc:1 + dc + Wd]
                            outv = cv[:, 0:Ts].rearrange("p (r w) -> p r w", w=Wd)
                        else:
                            rhs = U[:, 1 + dc:1 + dc + Wd]
                            outv = cv[:, 0:Ts]
                        self.op(PE, lambda e, outv=outv, rhs=rhs, dg=dg, half=half, tap=tap, ti=ti, nt_=len(taps): e.matmul(
                            outv, lhsT=dg[:, half, tap, :], rhs=rhs, start=(ti == 0), stop=(ti == nt_ - 1)),
                            reads=[r_dg, r_U], writes=[r_cv], flag=(ti == len(taps) - 1))
                    cvs.append((cv, r_cv))
                sgl, r_sgl, _ = B["sgl"].next()
                self.op(ACT, lambda e, sgl=sgl, cv=cvs[1][0]: e.activation(out=sgl[:, 0:Ts], in_=cv[:, 0:Ts],
                                                                           func=AF.Silu), reads=[cvs[1][1]],
                        writes=[r_sgl])
                self.op(DVE, lambda e, sgl=sgl, cv=cvs[0][0], act=act, j=j: e.tensor_tensor(
                    out=act[:, j, 0:Ts], in0=cv[:, 0:Ts], in1=sgl[:, 0:Ts], op=ALU.mult), reads=[cvs[0][1], r_sgl],
                    writes=[r_act])
            if s + 1 < nst:
                load_hw(s + 1)
            for ts in range(Ts // 128):
                ti = (r0 * Wd) // 128 + ts
                x1t, r_x1t, d_x1t = B["x1t"].next()
                self.op(SP, lambda e, x1t=x1t, ti=ti: e.dma_start(out=x1t[:], in_=X1[ti * 128:(ti + 1) * 128, :]),
                        reads=[rX1], writes=[r_x1t], dsem=d_x1t)
                t1, r_t1, _ = B["t1"].next()
                for nh in range(2):
                    Fp, r_F, _ = B["F"].next()
                    for j in range(NPAIR):
                        self.op(PE, lambda e, Fp=Fp, j=j, nh=nh, ts=ts, act=act: e.matmul(
                            Fp[:, :], lhsT=act[:, j, ts * 128:(ts + 1) * 128],
                            rhs=W["wdn"][:, j, nh * 512:(nh + 1) * 512], start=(j == 0), stop=(j == NPAIR - 1)),
                            reads=[r_act, W["r_wdn"]], writes=[r_F], flag=(j == NPAIR - 1))
                    self.op(DVE, lambda e, Fp=Fp, nh=nh, t1=t1: e.tensor_tensor(
                        out=t1[:, nh * 512:(nh + 1) * 512], in0=Fp[:, :],
                        in1=self.grow[:, si, 1, nh * 512:(nh + 1) * 512], op=ALU.mult), reads=[r_F, self.r_grow],
                        writes=[r_t1])
                self.op(DVE, lambda e, t1=t1, x1t=x1t: e.scalar_tensor_tensor(
                    out=t1[:], in0=x1t[:], scalar=ALPHA, in1=t1[:], op0=ALU.mult, op1=ALU.add), reads=[r_x1t, r_t1],
                    writes=[r_t1])
                rd = self.r_out if xdst is self.out else (self.r_dram["XS"] if sq == "lat" else self.r_dram["CS"])
                for _ in self.residual_tail(B, W, t1, r_t1, 1, si, sq, ti, mod=False, dst=xdst, rdst=rd):
                    pass


def host_consts():
    s = np.arange(128)[:, None]
    t = np.arange(128)[None, :]
    same = (s // 64) == (t // 64)
    mcum_f = (same & (s <= t)).astype(np.float32)
    mcum_b = (same & (s >= t)).astype(np.float32)
    mrev_f = (same & (s > t)).astype(np.float32)
    mrev_b = (same & (s < t)).astype(np.float32)
    mcum = np.stack([np.tile(mcum_f, (1, 4)), np.tile(mcum_b, (1, 4))], axis=1)
    mrev = np.stack([mrev_f, mrev_b], axis=1)
    ind = np.zeros((128, 64), np.float32)
    ind[0:64, 0] = 1.0
    ind[64:128, 1] = 1.0
    hm = (np.arange(128)[:, None] // 32 == np.arange(4)[None, :]).astype(np.float32)
    bm1 = (np.arange(128)[:, None] // 32 == (np.arange(256)[None, :] // 64)).astype(np.float32)
    bm = np.tile(bm1, (1, 2))
    return dict(identf=np.eye(128, dtype=np.float32), mcum=np.ascontiguousarray(mcum),
                mrev=np.ascontiguousarray(mrev), ind=ind, hm=hm, bm=np.ascontiguousarray(bm),
                onesf=np.ones((128, 128), np.float32))


def pool_consts(Ls_list):
    wins = (2, 4, 8, 16)
    pbm = np.zeros((128, 3, 4, 128), np.float32)
    pbp = np.zeros((128, 4, 128), np.float32)
    pbn = np.zeros((128, 4, 128), np.float32)
    Lbig = 128 * 5
    for gi, w in enumerate(wins):
        for var, tile in ((0, 0), (1, 2), (2, 4)):
            for t in range(128):
                tg = tile * 128 + t
                lo, hi = max(tg - w // 2, 0), min(tg + w - w // 2, Lbig)
                for sg in range(lo, hi):
                    sl = sg - tile * 128
                    val = 1.0 / (hi - lo)
                    if 0 <= sl < 128:
                        pbm[sl, var, gi, t] += val
                    elif var == 1 and sl < 0:
                        pbp[128 + sl, gi, t] += val
                    elif var == 1 and sl >= 128:
                        pbn[sl - 128, gi, t] += val
                pbm[t, var, gi, t] -= 1.0
    return dict(pbm=pbm, pbp=pbp, pbn=pbn)


def layer_inputs(l, p):
    f = lambda a: np.ascontiguousarray(a, dtype=np.float32)
    pre = "l%d_" % l
    o = {}
    wm = p["w_mod"][l].reshape(8, 128, 6, D)
    o["wmod"] = f(wm.transpose(2, 1, 0, 3).reshape(6, 128, 8 * D))
    bm_ = p["b_mod"][l].reshape(6, 8, 128)
    o["bmodT"] = f(bm_.transpose(2, 0, 1))
    o["brow"] = f(np.broadcast_to(np.stack([bm_[2].reshape(D), bm_[5].reshape(D)])[None], (128, 2, D)))
    wi = p["w_in"][l]
    CA = 800
    q, k, v, r, low = wi[:, 0:128], wi[:, 128:256], wi[:, 256:512], wi[:, 512:768], wi[:, 768:800]
    zb = wi[:, CA:CA + 256]
    zcu, zcv = wi[:, CA + 256:CA + 512], wi[:, CA + 512:CA + 768]
    za, zg = wi[:, CA + 768:CA + 1024], wi[:, CA + 1024:CA + 1280]
    wcat = np.concatenate([k, v, zb, zcv, q, k, r, zcu, za, zg, low], axis=1)
    assert wcat.shape[1] == NWIN
    o["win"] = f(wcat.reshape(8, 128, NWIN).transpose(1, 0, 2))
    wg = np.zeros((33, 256), np.float32)
    wg[0:16, 0:128] = p["gla_w_gate"][l, 0]
    wg[16:32, 128:256] = p["gla_w_gate"][l, 1]
    wg[32, 0:128] = p["gla_b_gate"][l, 0]
    wg[32, 128:256] = p["gla_b_gate"][l, 1]
    o["wg"] = wg
    o["wout"] = f(p["w_out"][l].reshape(8, 128, D).transpose(1, 0, 2))
    o["wdn"] = f(p["ffn_w_down"][l].reshape(NPAIR, 128, D).transpose(1, 0, 2))
    pw = np.zeros((128, 2, 128), np.float32)
    for g in range(4):
        a = (g % 2) * 64
        pw[a:a + 64, g // 2, a:a + 64] = p["pool_w"][l, g]
    o["pw"] = pw
    o["sw"] = f(p["sgu_w"][l].transpose(2, 0, 1))
    o["sgub"] = f(p["sgu_b"][l].reshape(1, 512))
    o["convb"] = f(p["cm_conv_b"][l].reshape(1, 256))
    o["lnrow"] = f(np.broadcast_to(np.stack([p["sgu_ln_w"][l], p["sgu_ln_b"][l]])[None], (128, 2, 256)))
    pr = np.stack([p["post_ln_w"][l, 0], p["post_ln_b"][l, 0], p["post_ln_w"][l, 1], p["post_ln_b"][l, 1]])
    o["prow"] = f(np.broadcast_to(pr[None], (128, 4, D)))
    fv = np.zeros((128, 8), np.float32)
    for i_, nm in enumerate(("gla_norm_w", "pool_scale", "cm_ln_w", "cm_ln_b")):
        fv[:, 2 * i_:2 * i_ + 2] = p[nm][l].reshape(2, 128).T
    o["fvec"] = fv
    o["cw31"] = f(p["cm_conv_w"][l].reshape(31, 2, 128).transpose(2, 1, 0))
    o["cw9"] = f(p["ffn_conv_w"][l].reshape(9, 44, 128).transpose(2, 1, 0))
    wu = p["ffn_w_up"][l].reshape(8, 128, 2, NPAIR, 128)
    o["wup"] = f(wu.transpose(3, 1, 0, 2, 4).reshape(NPAIR, 128, 8 * 256))
    return {pre + k_: v_ for k_, v_ in o.items()}


PIPE = 2
LAG_A = 0
LAG_B = 6
_CACHE = {}
STOP = None
DEBUG = set()
LAST = {}


def run(inputs, L, ncores, bidx):
    key = (L,)
    if key not in _CACHE:
        kb = K(L)
        kb.stop = STOP
        kb.chain_i = 0
        kb.r_out = Res()
        nc = kb.build()
        _CACHE[key] = (kb, nc)
    kb, nc = _CACHE[key]
    p = {k_: np.asarray(v_, dtype=np.float32) for k_, v_ in inputs.items()}
    shared = {}
    shared.update(host_consts())
    shared.update(pool_consts(None))
    for l in range(DEPTH):
        shared.update(layer_inputs(l, p))
    in_maps = []
    for ci in range(ncores):
        b = bidx[ci]
        m = dict(shared)
        m["x"] = np.ascontiguousarray(p["x"][b])
        m["ctx"] = np.ascontiguousarray(p["ctx"][b])
        cT = np.stack([p["c"][b].reshape(8, 128).T, p["c_ctx"].reshape(8, 128).T], axis=-1)
        m["cT"] = np.ascontiguousarray(cT, dtype=np.float32)
        in_maps.append(m)
    res = run_bass_kernel_spmd(nc, in_maps, core_ids=list(range(ncores)))
    if DEBUG:
        LAST["res"] = res.results
    return [np.asarray(r["out"]) for r in res.results]


def kernel(**inputs):
    x = np.asarray(inputs["x"])
    Bn, L, _ = x.shape
    bidx = [i % Bn for i in range(8)]
    outs = run(inputs, L, 8, bidx)
    return np.stack(outs[:Bn], axis=0).astype(np.float32)
```

```python
import contextlib
import numpy as np
import concourse.bass as bass
import concourse.mybir as mybir
from concourse.bass_utils import run_bass_kernel_spmd

F32 = mybir.dt.float32
BF16 = mybir.dt.bfloat16
AF = mybir.ActivationFunctionType
ALU = mybir.AluOpType
AX = mybir.AxisListType

PE, ACT, DVE, POOL, SP = "pe", "act", "dve", "pool", "sp"
ENGS = (PE, ACT, DVE, POOL, SP)

D = 1024
GW = 64
LC = 256
DEPTH = 2
HID = 2816
NPAIR = 22
ALPHA = (2 * DEPTH) ** 0.25
EPS = 1e-6
NZT = 896
NZF = 1312
NWIN = NZT + NZF


class Res:
    __slots__ = ("w", "r")

    def __init__(self):
        self.w = None
        self.r = {}


class Prog:
    def __init__(self, nc):
        self.nc = nc
        self.ops = {e: [] for e in ENGS}
        self.sems = {}
        self.count = {}
        self.waited = {e: {} for e in ENGS}
        for e in ENGS:
            self.sems[e] = nc.alloc_semaphore(name="s_" + e)
            self.count[e] = 0
        self.nd = 0

    def dsem(self):
        k = "d%d" % self.nd
        self.nd += 1
        self.sems[k] = self.nc.alloc_semaphore(name="s_" + k)
        self.count[k] = 0
        return k

    def _need(self, eng, tok, waits):
        if tok is None:
            return
        k, v = tok
        if k == PE and eng == PE:
            return
        if k not in ENGS:
            v = max(v, self.count[k])
        if self.waited[eng].get(k, 0) >= v:
            return
        if waits.get(k, 0) < v:
            waits[k] = v

    def op(self, eng, fn, reads=(), writes=(), flag=True, dsem=None):
        waits = {}
        for r in reads:
            self._need(eng, r.w, waits)
        for w in writes:
            self._need(eng, w.w, waits)
            for t in w.r.items():
                self._need(eng, t, waits)
        for k, v in waits.items():
            self.waited[eng][k] = v
        if dsem is not None:
            self.count[dsem] += 16
            tok = (dsem, self.count[dsem])
            inc = (dsem, 16)
        elif flag:
            self.count[eng] += 1
            tok = (eng, self.count[eng])
            inc = (eng, 1)
        else:
            tok = (eng, self.count[eng] + 1)
            inc = None
        for r in reads:
            if r.r.get(tok[0], 0) < tok[1]:
                r.r[tok[0]] = tok[1]
        for w in writes:
            w.w = tok
            w.r = {}
        self.ops[eng].append((list(waits.items()), fn, inc))

    def barrier(self):
        tot = [(k, c) for k, c in self.count.items() if c > 0]
        for e in ENGS:
            w = [(k, c) for k, c in tot if self.waited[e].get(k, 0) < c]
            for k, c in w:
                self.waited[e][k] = c
            self.ops[e].append((w, None, None))

    def emit(self, final=False):
        nc = self.nc
        engmap = {PE: "tensor", ACT: "scalar", DVE: "vector", POOL: "gpsimd", SP: "sync"}
        endw = [(k, c) for k, c in self.count.items() if c > 0]
        with nc.Block() as block:
            for e in ENGS:
                ops = self.ops[e]
                if final and e == SP:
                    ops = ops + [(endw, None, None)]
                if not ops:
                    continue

                def body(engine, ops=ops):
                    for waits, fn, inc in ops:
                        for k, v in waits:
                            engine.wait_ge(self.sems[k], v)
                        if fn is None:
                            continue
                        ins = fn(engine)
                        if inc is not None:
                            ins.then_inc(self.sems[inc[0]], inc[1])

                getattr(block, engmap[e])(body)
        self.ops = {e: [] for e in ENGS}


class Ring:
    def __init__(self, K, name, shape, dt, n, dma=False, psum=False):
        self.t = []
        self.r = []
        self.d = []
        self.i = 0
        for j in range(n):
            if psum:
                self.t.append(K.ps("%s%d" % (name, j), dt))
            else:
                self.t.append(K.sb("%s%d" % (name, j), shape, dt))
            self.r.append(Res())
            self.d.append(K.P.dsem() if dma else None)

    def next(self):
        j = self.i % len(self.t)
        self.i += 1
        return self.t[j], self.r[j], self.d[j]


class StopBuild(Exception):
    pass


class K:
    stop = None

    def __init__(self, L, depth=DEPTH, R=8):
        self.L = L
        self.depth = depth
        self.R = R
        self.nc = bass.Bass("TRN2", target_bir_lowering=False)
        self.P = Prog(self.nc)
        self.inp = {}
        self.uid = 0
        self.scopes = []

    @contextlib.contextmanager
    def scope(self):
        with contextlib.ExitStack() as es:
            self.scopes.append(es)
            try:
                yield
            finally:
                self.scopes.pop()

    def chk(self, tag):
        if self.stop == tag:
            self.P.barrier()
            self.P.emit(final=True)
            raise StopBuild()

    def din(self, name, shape, dt=F32):
        self.inp[name] = (tuple(shape), dt)
        return self.nc.dram_tensor(name, list(shape), dt, kind="ExternalInput").ap()

    def dscr(self, name, shape, dt):
        kind = "ExternalOutput" if name in DEBUG else "Internal"
        return self.nc.dram_tensor(name, list(shape), dt, kind=kind).ap()

    def sb(self, name, shape, dt):
        self.uid += 1
        return self.scopes[-1].enter_context(self.nc.sbuf_tensor("%s_%d" % (name, self.uid), list(shape), dt))

    def ps(self, name, dt=F32):
        self.uid += 1
        return self.scopes[-1].enter_context(
            self.nc.psum_tensor("%s_%d" % (name, self.uid), [128, 512 if dt == F32 else 1024], dt))

    def op(self, *a, **k):
        self.P.op(*a, **k)

    def load(self, q, dst, src, res, dsem):
        self.op(q, lambda e: e.dma_start(out=dst, in_=src), writes=[res], dsem=dsem)

    def const(self, name, shape, dt=F32, cast=None, q=SP):
        src = self.din(name, shape)
        t = self.sb(name, shape, cast or dt)
        r = Res()
        nfree = int(np.prod(shape[1:]))
        if cast and nfree >= 4096:
            names = "abcd"[:len(shape) - 1]
            pat = "p %s -> p (%s)" % (" ".join(names), " ".join(names))
            tf = t[:].rearrange(pat) if len(shape) > 2 else t[:]
            sf = src.rearrange(pat) if len(shape) > 2 else src
            self.cast_cols(tf, sf, nfree, r)
        else:
            self.load(POOL if cast else q, t[:], src, r, self.cd)
        return t, r

    def cast_cols(self, dst2d, src2d, n, rdst):
        c = 0
        while c < n:
            w = min(1024, n - c)
            st, rs, ds = self.cst.next()
            self.op(SP, lambda e, st=st, c=c, w=w: e.dma_start(out=st[:, 0:w], in_=src2d[:, c:c + w]), writes=[rs],
                    dsem=ds)
            eng = ACT if (self.cast_i % 2 == 0) else DVE
            self.cast_i += 1
            if eng == ACT:
                self.op(ACT, lambda e, st=st, c=c, w=w: e.activation(out=dst2d[:, c:c + w], in_=st[:, 0:w],
                                                                     func=AF.Copy), reads=[rs], writes=[rdst])
            else:
                self.op(DVE, lambda e, st=st, c=c, w=w: e.tensor_copy(out=dst2d[:, c:c + w], in_=st[:, 0:w]),
                        reads=[rs], writes=[rdst])
            c += w

    def pipeline(self, nt, prefetch, tilegen, depth=2, lag=0):
        active = []
        nxt = 0
        while nxt < nt or active:
            while len(active) < depth and nxt < nt and (not active or active[-1][1] >= lag):
                prefetch(nxt)
                active.append([tilegen(nxt), 0])
                nxt += 1
            for ent in list(active):
                try:
                    next(ent[0])
                    ent[1] += 1
                except StopIteration:
                    active.remove(ent)

    def ln_stats(self, src, n, rsrc):
        nchk = max(1, n // 512)
        w = n // nchk
        st, rst, _ = self.r_st.next()
        mv, rmv, _ = self.r_mv.next()
        for c in range(nchk):
            self.op(DVE, lambda e, c=c: e.bn_stats(out=st[:, c * 6:(c + 1) * 6], in_=src[:, c * w:(c + 1) * w]),
                    reads=[rsrc], writes=[rst])
        self.op(DVE, lambda e: e.bn_aggr(out=mv[:, 0:2], in_=st[:, 0:nchk * 6]), reads=[rst], writes=[rmv])
        self.op(ACT, lambda e: e.activation(out=mv[:, 2:3], in_=mv[:, 1:2], func=AF.Sqrt, bias=EPS, scale=1.0),
                reads=[rmv], writes=[rmv])
        self.op(DVE, lambda e: e.reciprocal(out=mv[:, 3:4], in_=mv[:, 2:3]), reads=[rmv], writes=[rmv])
        self.op(DVE, lambda e: e.scalar_tensor_tensor(out=mv[:, 4:5], in0=mv[:, 0:1], scalar=-1.0, in1=mv[:, 3:4],
                                                      op0=ALU.mult, op1=ALU.mult), reads=[rmv], writes=[rmv])
        return mv[:, 3:4], mv[:, 4:5], rmv

    def build(self):
        self._root = contextlib.ExitStack()
        self.scopes.append(self._root)
        return self._build()

    def _build(self):
        nc, P, L = self.nc, self.P, self.L
        self.cd = P.dsem()
        x_in = self.din("x", [L, D])
        c_in = self.din("ctx", [LC, D])
        self.out = nc.dram_tensor("out", [L, D], F32, kind="ExternalOutput").ap()
        self.XS = self.dscr("XS", [L, D], F32)
        self.CS = self.dscr("CS", [LC, D], F32)
        self.X1 = {"lat": self.dscr("X1l", [L, D], F32), "ctx": self.dscr("X1c", [LC, D], F32)}
        self.H2T = {"lat": self.dscr("H2l", [8, 128, L], BF16), "ctx": self.dscr("H2c", [8, 128, LC], BF16)}
        self.ZT = {"lat": self.dscr("ZTl", [L, NZT], BF16), "ctx": self.dscr("ZTc", [LC, NZT], BF16)}
        self.ZF = {"lat": self.dscr("ZFl", [9, 128, L], BF16), "ctx": self.dscr("ZFc", [9, 128, LC], BF16)}
        self.G = {"lat": self.dscr("Gl", [L, 256], F32), "ctx": self.dscr("Gc", [LC, 256], F32)}
        self.WUPB = self.dscr("WUPB", [NPAIR, 128, 8 * 256], BF16)
        if "YD" in DEBUG:
            self.YD = self.dscr("YD", [8, 128, L], BF16)
            self.r_YD = Res()
            self.ydsem = P.dsem()
        self.r_dram = {k: Res() for k in ("XS", "CS", "X1lat", "X1ctx", "H2lat", "H2ctx", "ZTlat", "ZTctx",
                                           "ZFlat", "ZFctx", "Glat", "Gctx", "WUPB")}
        self.identb, self.r_identb = self.const("identf", [128, 128], cast=BF16)
        self.mcum, self.r_mcum = self.const("mcum", [128, 2, 512])
        self.mrev, self.r_mrev = self.const("mrev", [128, 2, 128])
        self.ind, self.r_ind = self.const("ind", [128, 64])
        self.hm, self.r_hm = self.const("hm", [128, 4])
        self.bm, self.r_bm = self.const("bm", [128, 512])
        self.onesf, self.r_onesf = self.const("onesf", [128, 128])
        self.pbm, self.r_pbm = self.const("pbm", [128, 3, 4, 128], cast=BF16)
        self.pbp, self.r_pbp = self.const("pbp", [128, 4, 128], cast=BF16)
        self.pbn, self.r_pbn = self.const("pbn", [128, 4, 128], cast=BF16)
        self.cT, self.r_cT = self.const("cT", [128, 8, 2])
        self.r_consts = [self.r_identb, self.r_mcum, self.r_mrev, self.r_ind, self.r_hm, self.r_bm, self.r_onesf,
                         self.r_pbm, self.r_pbp, self.r_pbn]
        self.scT = self.sb("scT", [128, 8, 2], F32)
        self.r_scT = Res()
        self.op(ACT, lambda e: e.activation(out=self.scT[:], in_=self.cT[:], func=AF.Silu), reads=[self.r_cT],
                writes=[self.r_scT])
        self.screp = self.sb("screp", [128, 2, 8, 128], F32)
        self.r_screp = Res()
        for s in range(2):
            for kc in range(8):
                self.op(ACT, lambda e, s=s, kc=kc: e.activation(out=self.screp[:, s, kc, :], in_=self.onesf[:],
                                                               func=AF.Copy, scale=self.scT[:, kc, s:s + 1]),
                        reads=[self.r_onesf, self.r_scT], writes=[self.r_screp])
        self.mT = self.sb("mT", [128, 2, 6, 8], F32)
        self.r_mT = Res()
        self.grow = self.sb("grow", [128, 2, 2, D], F32)
        self.r_grow = Res()
        self.cst = Ring(self, "cst", [128, 1024], F32, 2, dma=True)
        self.cast_i = 0
        self.r_st = Ring(self, "st", [128, 12], F32, 4)
        self.r_mv = Ring(self, "mv", [128, 8], F32, 4)
        nchl, nchc = L // 64, LC // 64
        self.nch = {"ctx": nchc, "lat": nchl}
        self.choff = {"ctx": 0, "lat": nchc}
        self.NCH = nchl + nchc
        self.r_SF, self.r_SB, self.r_KVB, self.r_stf = Res(), Res(), Res(), Res()
        self.prow = self.sb("prow", [128, 4, D], F32)
        self.r_prow = Res()
        dm = self.sb("dm", [128, 8], F32)
        r_dm = Res()
        self.op(POOL, lambda e: e.memset(dm[:], 1.0), writes=[r_dm])
        for bv in (0.0, EPS, 1.0):
            self.op(ACT, lambda e, bv=bv: e.activation(out=dm[:, 4:8], in_=dm[:, 0:4], func=AF.Identity, bias=bv,
                                                       scale=1.0), reads=[r_dm], writes=[r_dm])
        P.emit()
        xsrc, csrc = x_in, c_in
        try:
            for l in range(self.depth):
                last = (l == self.depth - 1)
                self.layer(l, xsrc, csrc, self.out if last else self.XS, self.CS, last)
                xsrc, csrc = self.XS, self.CS
        except StopBuild:
            pass
        return nc

    def layer(self, l, xsrc, csrc, xdst, cdst, last):
        nc, P = self.nc, self.P
        pre = "l%d_" % l
        NCH = self.NCH
        with self.scope():
            wst = Ring(self, pre + "wst", [128, 8, D], F32, 2, dma=True)
            bmodT, r_bmodT = self.const(pre + "bmodT", [128, 6, 8])
            brow, r_brow = self.const(pre + "brow", [128, 2, D])
            self.load(SP, self.prow[:], self.din(pre + "prow", [128, 4, D]), self.r_prow, self.cd)
            pm = self.ps(pre + "pm")
            r_pm = Res()
            pg = [self.ps(pre + "pg0"), self.ps(pre + "pg1")]
            r_pg = [Res(), Res()]
            wmod = self.din(pre + "wmod", [6, 128, 8 * D])
            for v in range(6):
                w, rw, dw = wst.next()
                self.load(SP, w[:].rearrange("p a b -> p (a b)"), wmod[v], rw, dw)
                if v in (2, 5):
                    gi = 0 if v == 2 else 1
                    for s in range(2):
                        for nh in range(2):
                            for kc in range(8):
                                self.op(PE, lambda e, s=s, nh=nh, kc=kc, w=w: e.matmul(
                                    pg[nh][:, :], lhsT=self.screp[:, s, kc, :], rhs=w[:, kc, nh * 512:(nh + 1) * 512],
                                    start=(kc == 0), stop=(kc == 7)), reads=[rw, self.r_screp], writes=[r_pg[nh]],
                                    flag=(kc == 7))
                            self.op(DVE, lambda e, s=s, nh=nh, gi=gi: e.tensor_tensor(
                                out=self.grow[:, s, gi, nh * 512:(nh + 1) * 512], in0=pg[nh][:, :],
                                in1=brow[:, gi, nh * 512:(nh + 1) * 512], op=ALU.add),
                                reads=[r_pg[nh], r_brow], writes=[self.r_grow])
                else:
                    for j in range(8):
                        for kc in range(8):
                            self.op(PE, lambda e, j=j, kc=kc, w=w: e.matmul(
                                pm[:, 2 * j:2 * j + 2], lhsT=w[:, kc, j * 128:(j + 1) * 128], rhs=self.scT[:, kc, :],
                                start=(kc == 0), stop=(kc == 7)), reads=[rw, self.r_scT], writes=[r_pm],
                                flag=(kc == 7))
                    for s in range(2):
                        self.op(DVE, lambda e, s=s, v=v: e.tensor_tensor(
                            out=self.mT[:, s, v, :], in0=pm[:, 0:16].rearrange("p (j s) -> p s j", s=2)[:, s, :],
                            in1=bmodT[:, v, :], op=ALU.add), reads=[r_pm, r_bmodT], writes=[self.r_mT])
                        if v in (1, 4):
                            self.op(DVE, lambda e, s=s, v=v: e.tensor_scalar(
                                out=self.mT[:, s, v, :], in0=self.mT[:, s, v, :], scalar1=1.0, scalar2=None,
                                op0=ALU.add), reads=[self.r_mT], writes=[self.r_mT])
            wupf = self.din(pre + "wup", [NPAIR, 128, 8 * 256])
            stg = Ring(self, pre + "stg", [128, 8 * 256], BF16, 2, dma=True)
            sd = P.dsem()
            for j in range(NPAIR):
                t, rt, dt_ = stg.next()
                self.cast_cols(t[:], wupf[j], 8 * 256, rt)
                self.op(SP, lambda e, j=j, t=t: e.dma_start(out=self.WUPB[j], in_=t[:]), reads=[rt],
                        writes=[self.r_dram["WUPB"]], dsem=sd)
            P.barrier()
            P.emit()
            self.chk("S%d" % l)
        W = dict(prow=self.prow, r_prow=self.r_prow)
        with self.scope():
            self.SF = self.sb("SF", [128, NCH, 64], BF16)
            self.SB_ = self.sb("SB", [128, NCH, 64], BF16)
            with self.scope():
                W["win"], W["r_win"] = self.const(pre + "win", [128, 8, NWIN], cast=BF16)
                W["wg"], W["r_wg"] = self.const(pre + "wg", [33, 256], cast=BF16)
                W["lnrow"], W["r_lnrow"] = self.const(pre + "lnrow", [128, 2, 256])
                self.KVB = self.sb("KVB", [128, NCH, 64], F32)
                self.DECB = self.sb("DECB", [128, NCH], F32)
                self.stf = self.sb("stf", [128, 2, 64], F32)
                self.chain_i = 0
                self.op(POOL, lambda e: e.memset(self.stf[:, 0, :], 0.0), writes=[self.r_stf])
                BA = self.bufsA(pre)
                self.passA(BA, W, "ctx", csrc, LC)
                self.passA(BA, W, "lat", xsrc, self.L)
                self.bscan(pre)
                P.barrier()
                P.emit()
                self.chk("A%d" % l)
            with self.scope():
                W["wout"], W["r_wout"] = self.const(pre + "wout", [128, 8, D], cast=BF16)
                W["pw"], W["r_pw"] = self.const(pre + "pw", [128, 2, 128], cast=BF16)
                W["sw"], W["r_sw"] = self.const(pre + "sw", [128, 4, 128], cast=BF16)
                W["sgub"], W["r_sgub"] = self.const(pre + "sgub", [1, 512])
                W["convb"], W["r_convb"] = self.const(pre + "convb", [1, 256])
                W["fvec"], W["r_fvec"] = self.const(pre + "fvec", [128, 8])
                cw31, r_cw31 = self.const(pre + "cw31", [128, 2, 31])
                d31 = self.sb(pre + "d31", [128, 2, 31, 128], BF16)
                r_d31 = Res()
                for b in range(2):
                    for j in range(31):
                        self.op(POOL, lambda e, b=b, j=j: e.tensor_scalar(out=d31[:, b, j, :], in0=self.identb[:],
                                                                         scalar1=cw31[:, b, j:j + 1], scalar2=None,
                                                                         op0=ALU.mult),
                                reads=[self.r_identb, r_cw31], writes=[r_d31])
                W["d31"], W["r_d31"] = d31, r_d31
                BB = self.bufsB(pre)
                if not last:
                    self.passB(BB, W, "ctx", csrc, LC)
                self.passB(BB, W, "lat", xsrc, self.L)
                P.barrier()
                P.emit()
                self.chk("B%d" % l)
        with self.scope():
            W["wdn"], W["r_wdn"] = self.const(pre + "wdn", [128, NPAIR, D], cast=BF16)
            W["cw9"], W["r_cw9"] = self.const(pre + "cw9", [128, 44, 9])
            BC = self.bufsC(pre)
            if not last:
                self.passC(BC, W, "ctx", cdst, 1, LC, 1)
            self.passC(BC, W, "lat", xdst, self.L // GW, GW, self.R)
            if not last:
                self.chk("C%d" % l)
            P.barrier()
            P.emit(final=last)

    def bufsA(self, pre):
        B = {}
        p2 = pre + "A_"
        B["xt"] = Ring(self, p2 + "xt", [128, D], F32, 2, dma=True)
        B["xn"] = Ring(self, p2 + "xn", [128, D], BF16, 2)
        B["hT"] = Ring(self, p2 + "hT", [128, 8, 128], BF16, 2)
        B["zt"] = Ring(self, p2 + "zt", [128, NZT], BF16, 2, dma=True)
        B["zf"] = Ring(self, p2 + "zf", [128, 9, 128], BF16, 2, dma=True)
        B["gv"] = Ring(self, p2 + "gv", [128, 256], F32, 2)
        B["gv2"] = Ring(self, p2 + "gv2", [128, 256], F32, 1)
        B["sg"] = Ring(self, p2 + "sg", [128, 2, 128], F32, 1)
        B["ee"] = Ring(self, p2 + "ee", [128, 256], F32, 1)
        B["g"] = Ring(self, p2 + "g", [128, 256], F32, 2, dma=True)
        B["e3"] = Ring(self, p2 + "e3", [128, 256], F32, 2)
        B["kdec"] = Ring(self, p2 + "kdec", [128, 2, 2, 128], BF16, 2)
        B["dec"] = Ring(self, p2 + "dec", [128, 4], F32, 2)
        B["kvm"] = Ring(self, p2 + "kvm", [128, 512], F32, 1)
        B["kvc"] = Ring(self, p2 + "kvc", [128, 2, 2, 64], F32, 2)
        B["TB"] = (self.ps(p2 + "TB", BF16), Res())
        for nm in ("Z0", "Z1", "F0", "F1", "F2", "M0", "M1"):
            B[nm] = (self.ps(p2 + nm), Res())
        for t_, r_ in zip(B["zf"].t, B["zf"].r):
            self.op(POOL, lambda e, t_=t_: e.memset(t_[32:33, 8, :], 1.0), writes=[r_])
        return B

    def passA(self, B, W, sq, xsrc, Ls):
        si = 0 if sq == "lat" else 1
        nt = Ls // 128
        win, r_win = W["win"], W["r_win"]
        TB, r_TB = B["TB"]
        Fb = [B["F0"], B["F1"], B["F2"]]
        xts = {}

        def ldx(i):
            xt, r_xt, d_xt = B["xt"].next()
            self.load(SP, xt[:], xsrc[i * 128:(i + 1) * 128, :], r_xt, d_xt)
            xts[i] = (xt, r_xt)

        def tileA(i):
                xt, r_xt = xts.pop(i)
                rstd, nb, r_mv = self.ln_stats(xt, D, r_xt)
                xn, r_xn, _ = B["xn"].next()
                self.op(ACT, lambda e, xn=xn, xt=xt, rstd=rstd, nb=nb: e.activation(
                    out=xn[:], in_=xt[:], func=AF.Identity, scale=rstd, bias=nb), reads=[r_xt, r_mv], writes=[r_xn])
                for kc in range(8):
                    self.op(PE, lambda e, kc=kc, xn=xn: e.transpose(TB[:, kc * 128:(kc + 1) * 128],
                                                                    xn[:, kc * 128:(kc + 1) * 128], self.identb[:]),
                            reads=[r_xn, self.r_identb], writes=[r_TB], flag=(kc == 7))
                hT, r_hT, _ = B["hT"].next()
                for kc in range(8):
                    if kc % 2 == 0:
                        self.op(ACT, lambda e, kc=kc, hT=hT: e.activation(
                            out=hT[:, kc, :], in_=TB[:, kc * 128:(kc + 1) * 128], func=AF.Identity,
                            scale=self.mT[:, si, 1, kc:kc + 1], bias=self.mT[:, si, 0, kc:kc + 1]),
                            reads=[r_TB, self.r_mT], writes=[r_hT])
                    else:
                        self.op(DVE, lambda e, kc=kc, hT=hT: e.tensor_scalar(
                            out=hT[:, kc, :], in0=TB[:, kc * 128:(kc + 1) * 128], scalar1=self.mT[:, si, 1, kc:kc + 1],
                            scalar2=self.mT[:, si, 0, kc:kc + 1], op0=ALU.mult, op1=ALU.add),
                            reads=[r_TB, self.r_mT], writes=[r_hT])
                self.chk("a1")
                yield
                (Z0, r_Z0), (Z1, r_Z1) = B["Z0"], B["Z1"]
                for (Z, rZ, c0, cn) in ((Z0, r_Z0, 0, 512), (Z1, r_Z1, 512, NZT - 512)):
                    for kc in range(8):
                        self.op(PE, lambda e, kc=kc, Z=Z, c0=c0, cn=cn, hT=hT: e.matmul(
                            Z[:, 0:cn], lhsT=hT[:, kc, :], rhs=win[:, kc, c0:c0 + cn], start=(kc == 0), stop=(kc == 7)),
                            reads=[r_hT, r_win], writes=[rZ], flag=(kc == 7))
                zt, r_zt, d_zt = B["zt"].next()
                self.op(ACT, lambda e, zt=zt: e.activation(out=zt[:, 0:512], in_=Z0[:, 0:512], func=AF.Copy),
                        reads=[r_Z0], writes=[r_zt])
                self.op(DVE, lambda e, zt=zt: e.tensor_copy(out=zt[:, 512:640], in_=Z1[:, 0:128]), reads=[r_Z1],
                        writes=[r_zt])
                gv, r_gv, _ = B["gv"].next()
                self.op(ACT, lambda e, gv=gv: e.activation(out=gv[:], in_=Z1[:, 128:384], func=AF.Gelu_apprx_tanh),
                        reads=[r_Z1], writes=[r_gv])
                rstd2, nb2, r_mv2 = self.ln_stats(gv, 256, r_gv)
                gv2, r_gv2, _ = B["gv2"].next()
                self.op(ACT, lambda e, gv=gv, gv2=gv2, rstd2=rstd2, nb2=nb2: e.activation(
                    out=gv2[:], in_=gv[:], func=AF.Identity, scale=rstd2, bias=nb2), reads=[r_gv, r_mv2], writes=[r_gv2])
                self.op(DVE, lambda e, gv2=gv2: e.tensor_tensor(out=gv2[:], in0=gv2[:], in1=W["lnrow"][:, 0, :],
                                                                op=ALU.mult), reads=[r_gv2, W["r_lnrow"]], writes=[r_gv2])
                self.op(DVE, lambda e, gv2=gv2, zt=zt: e.tensor_tensor(out=zt[:, 640:896], in0=gv2[:],
                                                                       in1=W["lnrow"][:, 1, :], op=ALU.add),
                        reads=[r_gv2, W["r_lnrow"]], writes=[r_zt])
                self.op(SP, lambda e, zt=zt, i=i: e.dma_start(out=self.ZT[sq][i * 128:(i + 1) * 128, :], in_=zt[:]),
                        reads=[r_zt], writes=[self.r_dram["ZT" + sq]], dsem=d_zt)
                self.chk("a2")
                yield
                for blk in range(11):
                    Fp, rF = Fb[blk // 4]
                    m = 32 if blk == 10 else 128
                    c0 = NZT + blk * 128
                    for kc in range(8):
                        self.op(PE, lambda e, kc=kc, Fp=Fp, blk=blk, m=m, c0=c0, hT=hT: e.matmul(
                            Fp[0:m, (blk % 4) * 128:(blk % 4 + 1) * 128], lhsT=win[:, kc, c0:c0 + m], rhs=hT[:, kc, :],
                            start=(kc == 0), stop=(kc == 7)), reads=[r_hT, r_win], writes=[rF], flag=(kc == 7))
                zf, r_zf, d_zf = B["zf"].next()
                F0, F1, F2 = Fb[0][0], Fb[1][0], Fb[2][0]
                rF0, rF1, rF2 = Fb[0][1], Fb[1][1], Fb[2][1]
                self.op(ACT, lambda e, zf=zf: e.activation(out=zf[:, 0, :], in_=F0[:, 0:128], func=AF.Copy,
                                                           scale=32.0 ** -0.5), reads=[rF0], writes=[r_zf])
                self.op(DVE, lambda e, zf=zf: e.tensor_copy(out=zf[:, 1, :], in_=F0[:, 128:256]), reads=[rF0],
                        writes=[r_zf])
                self.op(ACT, lambda e, zf=zf: e.activation(out=zf[:, 2:4, :].rearrange("p a b -> p (a b)"),
                                                           in_=F0[:, 256:512], func=AF.Silu), reads=[rF0], writes=[r_zf])
                self.op(ACT, lambda e, zf=zf: e.activation(out=zf[:, 4:6, :].rearrange("p a b -> p (a b)"),
                                                           in_=F1[:, 0:256], func=AF.Gelu_apprx_tanh), reads=[rF1],
                        writes=[r_zf])
                sg, r_sg, _ = B["sg"].next()
                self.op(ACT, lambda e, sg=sg: e.activation(out=sg[:].rearrange("p a b -> p (a b)"), in_=F2[:, 0:256],
                                                           func=AF.Sigmoid), reads=[rF2], writes=[r_sg])
                self.op(DVE, lambda e, sg=sg, zf=zf: e.tensor_tensor(out=zf[:, 6:8, :].rearrange("p a b -> p (a b)"),
                                                                     in0=F1[:, 256:512],
                                                                     in1=sg[:].rearrange("p a b -> p (a b)"), op=ALU.mult),
                        reads=[rF1, r_sg], writes=[r_zf])
                self.op(DVE, lambda e, zf=zf: e.tensor_copy(out=zf[0:32, 8, :], in_=F2[0:32, 256:384]), reads=[rF2],
                        writes=[r_zf])
                self.op(SP, lambda e, zf=zf, i=i: e.dma_start(
                    out=self.ZF[sq][:, :, i * 128:(i + 1) * 128].rearrange("b p t -> p b t"), in_=zf[:]),
                    reads=[r_zf], writes=[self.r_dram["ZF" + sq]], dsem=d_zf)
                self.chk("a3")
                yield
                (M0, r_M0), (M1, r_M1) = B["M0"], B["M1"]
                self.op(PE, lambda e, zf=zf: e.matmul(M0[:, 0:256], lhsT=zf[0:33, 8, :], rhs=W["wg"][0:33, :], start=True,
                                                      stop=True), reads=[r_zf, W["r_wg"]], writes=[r_M0])
                ee, r_ee, _ = B["ee"].next()
                self.op(ACT, lambda e, ee=ee: e.activation(out=ee[:], in_=M0[:, 0:256], func=AF.Exp, scale=-1.0),
                        reads=[r_M0], writes=[r_ee])
                self.op(ACT, lambda e, ee=ee: e.activation(out=ee[:], in_=ee[:], func=AF.Ln, bias=1.0, scale=1.0),
                        reads=[r_ee], writes=[r_ee])
                g, r_g, d_g = B["g"].next()
                self.op(DVE, lambda e, ee=ee, g=g: e.tensor_scalar(out=g[:], in0=ee[:], scalar1=-1.0 / 16.0, scalar2=-1.0,
                                                                   op0=ALU.mult, op1=ALU.max), reads=[r_ee], writes=[r_g])
                self.op(SP, lambda e, g=g, i=i: e.dma_start(out=self.G[sq][i * 128:(i + 1) * 128, :], in_=g[:]),
                        reads=[r_g], writes=[self.r_dram["G" + sq]], dsem=d_g)
                self.chk("a4")
                yield
                for dr in range(2):
                    self.op(PE, lambda e, dr=dr, g=g: e.matmul(M0[:, 256 + dr * 128:384 + dr * 128],
                                                               lhsT=self.mrev[:, dr, :], rhs=g[:, dr * 128:(dr + 1) * 128],
                                                               start=True, stop=True), reads=[r_g, self.r_mrev],
                            writes=[r_M0])
                    self.op(PE, lambda e, dr=dr, g=g: e.matmul(F2[:, 384 + 64 * dr:448 + 64 * dr],
                                                               lhsT=g[:, dr * 128:(dr + 1) * 128], rhs=self.ind[:],
                                                               start=True, stop=True), reads=[r_g, self.r_ind],
                            writes=[rF2])
                e3, r_e3, _ = B["e3"].next()
                self.op(ACT, lambda e, e3=e3: e.activation(out=e3[:], in_=M0[:, 256:512], func=AF.Exp), reads=[r_M0],
                        writes=[r_e3])
                dec, r_dec, _ = B["dec"].next()
                for dr in range(2):
                    self.op(ACT, lambda e, dec=dec, dr=dr: e.activation(out=dec[:, 2 * dr:2 * dr + 2],
                                                                        in_=F2[:, 384 + 64 * dr:386 + 64 * dr],
                                                                        func=AF.Exp), reads=[rF2], writes=[r_dec])
                self.chk("a41")
                yield
                kdec, r_kdec, _ = B["kdec"].next()
                for dr in range(2):
                    for j in range(2):
                        self.op(DVE, lambda e, dr=dr, j=j, kdec=kdec, e3=e3, zt=zt: e.scalar_tensor_tensor(
                            out=kdec[:, dr, j, :], in0=zt[:, 0:128], scalar=self.ind[:, j:j + 1],
                            in1=e3[:, dr * 128:(dr + 1) * 128], op0=ALU.mult, op1=ALU.mult),
                            reads=[r_zt, r_e3, self.r_ind], writes=[r_kdec])
                self.chk("a42")
                yield
                kvc, r_kvc, _ = B["kvc"].next()
                for dr in range(2):
                    for j in range(2):
                        self.op(PE, lambda e, dr=dr, j=j, kdec=kdec, zt=zt: e.matmul(
                            M1[:, j * 256:(j + 1) * 256], lhsT=kdec[:, dr, j, :],
                            rhs=zt[:, 128:384], start=True, stop=True), reads=[r_kdec, r_zt],
                            writes=[r_M1])
                    self.chk("a43")
                    kvm, r_kvm, _ = B["kvm"].next()
                    self.op(DVE, lambda e, kvm=kvm: e.tensor_tensor(out=kvm[:], in0=M1[:, :], in1=self.bm[:], op=ALU.mult),
                            reads=[r_M1, self.r_bm], writes=[r_kvm])
                    kv4 = kvm[:].rearrange("p (j h e) -> p j h e", j=2, h=4)
                    self.op(DVE, lambda e, kv4=kv4: e.tensor_tensor(out=kv4[:, :, 0, :], in0=kv4[:, :, 0, :],
                                                                    in1=kv4[:, :, 1, :], op=ALU.add),
                            reads=[r_kvm], writes=[r_kvm])
                    self.op(DVE, lambda e, kv4=kv4: e.tensor_tensor(out=kv4[:, :, 2, :], in0=kv4[:, :, 2, :],
                                                                    in1=kv4[:, :, 3, :], op=ALU.add),
                            reads=[r_kvm], writes=[r_kvm])
                    self.op(DVE, lambda e, kv4=kv4, kvc=kvc, dr=dr: e.tensor_tensor(
                        out=kvc[:, dr, :, :], in0=kv4[:, :, 0, :], in1=kv4[:, :, 2, :], op=ALU.add),
                        reads=[r_kvm], writes=[r_kvc])
                self.chk("a5")
                yield
                for j in range(2):
                    n = self.choff[sq] + 2 * i + j
                    cur = self.chain_i % 2
                    self.op(POOL, lambda e, n=n, cur=cur: e.tensor_copy(out=self.SF[:, n, :], in_=self.stf[:, cur, :]),
                            reads=[self.r_stf], writes=[self.r_SF])
                    self.op(DVE, lambda e, j=j, cur=cur, dec=dec, kvc=kvc: e.scalar_tensor_tensor(
                        out=self.stf[:, 1 - cur, :], in0=self.stf[:, cur, :], scalar=dec[:, j:j + 1], in1=kvc[:, 0, j, :],
                        op0=ALU.mult, op1=ALU.add), reads=[self.r_stf, r_dec, r_kvc], writes=[self.r_stf])
                    self.chain_i += 1
                    self.op(POOL, lambda e, n=n, j=j, kvc=kvc: e.tensor_copy(out=self.KVB[:, n, :], in_=kvc[:, 1, j, :]),
                            reads=[r_kvc], writes=[self.r_KVB])
                    self.op(POOL, lambda e, n=n, j=j, dec=dec: e.tensor_copy(out=self.DECB[:, n:n + 1],
                                                                             in_=dec[:, 2 + j:3 + j]),
                            reads=[r_dec], writes=[self.r_KVB])

        self.pipeline(nt, ldx, tileA, depth=PIPE, lag=LAG_A)

    def bscan(self, pre):
        order = list(range(self.nch["ctx"] - 1, -1, -1)) + \
            [self.choff["lat"] + n for n in range(self.nch["lat"] - 1, -1, -1)]
        self.op(POOL, lambda e: e.memset(self.stf[:, 0, :], 0.0), writes=[self.r_stf])
        cur = 0
        for n in order:
            self.op(POOL, lambda e, n=n, cur=cur: e.tensor_copy(out=self.SB_[:, n, :], in_=self.stf[:, cur, :]),
                    reads=[self.r_stf], writes=[self.r_SB])
            self.op(DVE, lambda e, n=n, cur=cur: e.scalar_tensor_tensor(
                out=self.stf[:, 1 - cur, :], in0=self.stf[:, cur, :], scalar=self.DECB[:, n:n + 1],
                in1=self.KVB[:, n, :], op0=ALU.mult, op1=ALU.add), reads=[self.r_stf, self.r_KVB],
                writes=[self.r_stf])
            cur = 1 - cur

    def bufsB(self, pre):
        B = {}
        p2 = pre + "B_"
        B["zt"] = Ring(self, p2 + "zt", [128, NZT], BF16, 4, dma=True)
        B["zf"] = Ring(self, p2 + "zf", [128, 9, 128], BF16, 2, dma=True)
        B["yw"] = Ring(self, p2 + "yw", [128, 2, 160], BF16, 2, dma=True)
        B["g"] = Ring(self, p2 + "g", [128, 256], F32, 2, dma=True)
        B["xt"] = Ring(self, p2 + "xt", [128, D], F32, 2, dma=True)
        B["e1"] = Ring(self, p2 + "e1", [128, 2, 128], F32, 2)
        B["e2"] = Ring(self, p2 + "e2", [128, 2, 128], F32, 2)
        B["kin"] = Ring(self, p2 + "kin", [128, 2, 128], BF16, 2)
        B["qm"] = Ring(self, p2 + "qm", [128, 2, 4, 128], BF16, 2)
        B["atm"] = Ring(self, p2 + "atm", [128, 2, 512], BF16, 2)
        B["sqo"] = Ring(self, p2 + "sqo", [128, 256], F32, 2)
        B["ssq"] = Ring(self, p2 + "ssq", [128, 8], F32, 2)
        B["on"] = Ring(self, p2 + "on", [128, 256], BF16, 2)
        B["yT"] = Ring(self, p2 + "yT", [128, 8, 128], BF16, 2)
        B["pp"] = Ring(self, p2 + "pp", [128, 2, 128], BF16, 2)
        B["cn"] = Ring(self, p2 + "cn", [128, 256], BF16, 2)
        B["t1"] = Ring(self, p2 + "t1", [128, D], F32, 2)
        B["x1"] = Ring(self, p2 + "x1", [128, D], F32, 2, dma=True)
        B["hn"] = Ring(self, p2 + "hn", [128, D], BF16, 2)
        B["h2"] = Ring(self, p2 + "h2", [128, 8, 128], BF16, 2, dma=True)
        for nm in ("b0", "b1", "b2", "b3", "b4", "b6", "b7"):
            B[nm] = self.ps(p2 + nm)
        B["b5"] = self.ps(p2 + "b5", BF16)
        for nm in ("cum", "AT0", "AT1", "pl", "sg", "tr", "Y0", "Y1"):
            B["r_" + nm] = Res()
        B["r_O"] = B["r_cum"]
        B["r_po"] = B["r_pl"]
        B["r_cv"] = B["r_Y0"]
        return B

    def passB(self, B, W, sq, xsrc, Ls):
        si = 0 if sq == "lat" else 1
        nt = Ls // 128
        b0, b1, b2, b3, b4, b5, b6, b7 = (B["b%d" % k] for k in range(8))
        AT = [b1, b2]
        r_AT = [B["r_AT0"], B["r_AT1"]]
        mT, r_mT = self.mT, self.r_mT
        fvec, r_fvec = W["fvec"], W["r_fvec"]
        ZT, ZF, G = self.ZT[sq], self.ZF[sq], self.G[sq]
        rZT, rZF, rG = self.r_dram["ZT" + sq], self.r_dram["ZF" + sq], self.r_dram["G" + sq]
        zts = {}
        lds = {}

        def load_zt(i):
            t, r, d = B["zt"].next()
            self.op(SP, lambda e: e.dma_start(out=t[:], in_=ZT[i * 128:(i + 1) * 128, :]), reads=[rZT], writes=[r],
                    dsem=d)
            zts[i] = (t, r)

        def load_rest(i):
            zf, r_zf, d_zf = B["zf"].next()
            self.op(SP, lambda e: e.dma_start(
                out=zf[:], in_=ZF[:, :, i * 128:(i + 1) * 128].rearrange("b p t -> p b t")), reads=[rZF],
                writes=[r_zf], dsem=d_zf)
            yw, r_yw, d_yw = B["yw"].next()
            lo, hi = max(i * 128 - 16, 0), min(i * 128 + 144, Ls)
            if lo > i * 128 - 16 or hi < i * 128 + 144:
                self.op(POOL, lambda e: e.memset(yw[:], 0.0), writes=[r_yw])
            o0 = lo - (i * 128 - 16)
            self.op(SP, lambda e: e.dma_start(
                out=yw[:, :, o0:o0 + hi - lo], in_=ZF[6:8, :, lo:hi].rearrange("b p t -> p b t")), reads=[rZF],
                writes=[r_yw], dsem=d_yw)
            g, r_g, d_g = B["g"].next()
            self.op(SP, lambda e: e.dma_start(out=g[:], in_=G[i * 128:(i + 1) * 128, :]), reads=[rG],
                    writes=[r_g], dsem=d_g)
            xt, r_xt, d_xt = B["xt"].next()
            self.load(SP, xt[:], xsrc[i * 128:(i + 1) * 128, :], r_xt, d_xt)
            lds[i] = (zf, r_zf, yw, r_yw, g, r_g, xt, r_xt)

        load_zt(0)

        def prefB(n):
            if n + 1 < nt:
                load_zt(n + 1)
            load_rest(n)

        def tileB(i):
                zt, r_zt = zts[i]
                zf, r_zf, yw, r_yw, g, r_g, xt, r_xt = lds.pop(i)
                yT, r_yT, _ = B["yT"].next()
                for dr in range(2):
                    self.op(PE, lambda e, dr=dr, g=g: e.matmul(b0[:, dr * 128:(dr + 1) * 128],
                                                               lhsT=g[:, dr * 128:(dr + 1) * 128],
                                                               rhs=self.mcum[:, dr, 0:128], start=True, stop=True),
                            reads=[r_g, self.r_mcum], writes=[B["r_cum"]])
                e1, r_e1, _ = B["e1"].next()
                e2, r_e2, _ = B["e2"].next()
                self.op(ACT, lambda e, e1=e1: e.activation(out=e1[:].rearrange("p a b -> p (a b)"), in_=b0[:, 0:256],
                                                           func=AF.Exp), reads=[B["r_cum"]], writes=[r_e1])
                self.op(ACT, lambda e, e2=e2: e.activation(out=e2[:].rearrange("p a b -> p (a b)"), in_=b0[:, 0:256],
                                                           func=AF.Exp, scale=-1.0), reads=[B["r_cum"]], writes=[r_e2])
                kin, r_kin, _ = B["kin"].next()
                qm, r_qm, _ = B["qm"].next()
                for dr in range(2):
                    self.op(DVE, lambda e, dr=dr, kin=kin, zf=zf, e2=e2: e.tensor_tensor(
                        out=kin[:, dr, :], in0=zf[:, 1, :], in1=e2[:, dr, :], op=ALU.mult), reads=[r_zf, r_e2],
                        writes=[r_kin])
                    for h in range(4):
                        eng = DVE
                        self.op(eng, lambda e, dr=dr, h=h, qm=qm, zf=zf, e1=e1: e.scalar_tensor_tensor(
                            out=qm[:, dr, h, :], in0=zf[:, 0, :], scalar=self.hm[:, h:h + 1], in1=e1[:, dr, :],
                            op0=ALU.mult, op1=ALU.mult), reads=[r_zf, r_e1, self.r_hm], writes=[r_qm])
                self.chk("b1")
                yield
                atm, r_atm, _ = B["atm"].next()
                for dr in range(2):
                    for h in range(4):
                        self.op(PE, lambda e, dr=dr, h=h, kin=kin, qm=qm: e.matmul(
                            AT[dr][:, h * 128:(h + 1) * 128], lhsT=kin[:, dr, :], rhs=qm[:, dr, h, :], start=True,
                            stop=True), reads=[r_kin, r_qm], writes=[r_AT[dr]], flag=(h == 3))
                    self.op(DVE, lambda e, dr=dr, atm=atm: e.tensor_tensor(out=atm[:, dr, :], in0=AT[dr][:, :],
                                                                           in1=self.mcum[:, dr, :], op=ALU.mult),
                            reads=[r_AT[dr], self.r_mcum], writes=[r_atm])
                self.chk("b2")
                yield
                SS = [(self.SF, self.r_SF), (self.SB_, self.r_SB)]
                for h in range(4):
                    for dr in range(2):
                        self.op(PE, lambda e, dr=dr, h=h, atm=atm, zt=zt: e.matmul(
                            b0[:, 256 + 64 * h:320 + 64 * h], lhsT=atm[:, dr, h * 128:(h + 1) * 128],
                            rhs=zt[:, 128 + 64 * h:192 + 64 * h], start=(dr == 0), stop=False, skip_group_check=True),
                            reads=[r_atm, r_zt], writes=[B["r_O"]], flag=False)
                        for j in range(2):
                            n = self.choff[sq] + 2 * i + j
                            lastmm = (dr == 1 and j == 1)
                            self.op(PE, lambda e, dr=dr, h=h, j=j, n=n, qm=qm, lastmm=lastmm: e.matmul(
                                b0[64 * j:64 * j + 64, 256 + 64 * h:320 + 64 * h], lhsT=qm[:, dr, h, 64 * j:64 * j + 64],
                                rhs=SS[dr][0][:, n, :], start=False, stop=lastmm, skip_group_check=True),
                                reads=[r_qm, SS[dr][1]], writes=[B["r_O"]], flag=(lastmm and h == 3))
                self.chk("b3")
                sqo, r_sqo, _ = B["sqo"].next()
                ssq, r_ssq, _ = B["ssq"].next()
                self.op(ACT, lambda e, sqo=sqo: e.activation(out=sqo[:], in_=b0[:, 256:512], func=AF.Square),
                        reads=[B["r_O"]], writes=[r_sqo])
                self.op(DVE, lambda e, sqo=sqo, ssq=ssq: e.tensor_reduce(
                    out=ssq[:, 0:4], in_=sqo[:].rearrange("p (h e) -> p h e", h=4), axis=AX.X, op=ALU.add),
                    reads=[r_sqo], writes=[r_ssq])
                self.op(ACT, lambda e, ssq=ssq: e.activation(out=ssq[:, 0:4], in_=ssq[:, 0:4], func=AF.Sqrt,
                                                             scale=1.0 / 64.0, bias=EPS), reads=[r_ssq], writes=[r_ssq])
                self.op(DVE, lambda e, ssq=ssq: e.reciprocal(out=ssq[:, 4:8], in_=ssq[:, 0:4]), reads=[r_ssq],
                        writes=[r_ssq])
                on, r_on, _ = B["on"].next()
                for h in range(4):
                    self.op(DVE, lambda e, h=h, on=on, ssq=ssq: e.tensor_scalar(
                        out=on[:, 64 * h:64 * h + 64], in0=b0[:, 256 + 64 * h:320 + 64 * h], scalar1=ssq[:, 4 + h:5 + h],
                        scalar2=None, op0=ALU.mult), reads=[B["r_O"], r_ssq], writes=[r_on])
                for blk in range(2):
                    self.op(PE, lambda e, blk=blk, on=on: e.transpose(b5[:, blk * 128:(blk + 1) * 128],
                                                                      on[:, blk * 128:(blk + 1) * 128], self.identb[:]),
                            reads=[r_on, self.r_identb], writes=[B["r_tr"]], flag=(blk == 1))
                for blk in range(2):
                    self.op(DVE, lambda e, blk=blk, yT=yT, zf=zf: e.scalar_tensor_tensor(
                        out=yT[:, blk, :], in0=b5[:, blk * 128:(blk + 1) * 128], scalar=fvec[:, blk:blk + 1],
                        in1=zf[:, 2 + blk, :], op0=ALU.mult, op1=ALU.mult), reads=[B["r_tr"], r_fvec, r_zf],
                        writes=[r_yT])
                self.chk("b4")
                yield
                var = 0 if i == 0 else (2 if i == nt - 1 else 1)
                if nt == 1:
                    var = 0
                for gi in range(4):
                    po_ = b3[64 * (gi % 2):64 * (gi % 2) + 64, (gi // 2) * 128:(gi // 2 + 1) * 128]
                    parts = [(zt, r_zt, 0, 128, self.pbm[:, var, gi, :], self.r_pbm)]
                    if i > 0:
                        parts.append((zts[i - 1][0], zts[i - 1][1], 0, 128, self.pbp[:, gi, :], self.r_pbp))
                    if i + 1 < nt:
                        parts.append((zts[i + 1][0], zts[i + 1][1], 0, 128, self.pbn[:, gi, :], self.r_pbn))
                    for k_, (zz, rzz, p0, p1, rhs, rr) in enumerate(parts):
                        self.op(PE, lambda e, po_=po_, zz=zz, p0=p0, p1=p1, rhs=rhs, gi=gi, k_=k_, np_=len(parts): e.matmul(
                            po_, lhsT=zz[p0:p1, 384 + 64 * gi:448 + 64 * gi], rhs=rhs, start=(k_ == 0),
                            stop=(k_ == np_ - 1)), reads=[rzz, rr], writes=[B["r_pl"]],
                            flag=(gi == 3 and k_ == len(parts) - 1))
                pp, r_pp, _ = B["pp"].next()
                self.op(ACT, lambda e, pp=pp: e.activation(out=pp[:].rearrange("p a b -> p (a b)"), in_=b3[:, 0:256],
                                                           func=AF.Copy), reads=[B["r_pl"]], writes=[r_pp])
                for gp in range(2):
                    self.op(PE, lambda e, gp=gp, pp=pp: e.matmul(b3[:, 256 + gp * 128:384 + gp * 128], lhsT=W["pw"][:, gp, :],
                                                                 rhs=pp[:, gp, :], start=True, stop=True),
                            reads=[r_pp, W["r_pw"]], writes=[B["r_po"]], flag=(gp == 1))
                for gp in range(2):
                    self.op(ACT, lambda e, gp=gp, yT=yT: e.activation(out=yT[:, 2 + gp, :],
                                                                     in_=b3[:, 256 + gp * 128:384 + gp * 128],
                                                                     func=AF.Copy, scale=fvec[:, 2 + gp:3 + gp]),
                            reads=[B["r_po"], r_fvec], writes=[r_yT])
                self.chk("b5")
                yield
                for h in range(4):
                    so_ = b4[64 * (h % 2):64 * (h % 2) + 64, (h // 2) * 128:(h // 2 + 1) * 128]
                    import os
                    NOSB = os.environ.get("SGUBIAS", "1") == "0"
                    self.op(PE, lambda e, so_=so_, h=h, zt=zt: e.matmul(so_, lhsT=zt[:, 640 + 64 * h:704 + 64 * h],
                                                                        rhs=W["sw"][:, h, :], start=True, stop=NOSB),
                            reads=[r_zt, W["r_sw"]], writes=[B["r_sg"]], flag=(NOSB and h == 3))
                    if NOSB:
                        continue
                    self.op(PE, lambda e, so_=so_, h=h: e.matmul(so_, lhsT=self.onesf[0:1, 0:64],
                                                                 rhs=W["sgub"][0:1, h * 128:(h + 1) * 128], start=False,
                                                                 stop=True), reads=[self.r_onesf, W["r_sgub"]],
                            writes=[B["r_sg"]], flag=(h == 3))
                for hp in range(2):
                    self.op(DVE, lambda e, hp=hp, yT=yT, zf=zf: e.tensor_tensor(
                        out=yT[:, 4 + hp, :], in0=b4[:, hp * 128:(hp + 1) * 128], in1=zf[:, 4 + hp, :], op=ALU.mult),
                        reads=[B["r_sg"], r_zf], writes=[r_yT])
                self.chk("b6")
                yield
                import os
                CV = os.environ.get("CONVVAR", "")
                for blk in range(2):
                    for j in range({"few": 8, "one": 1, "none": 0, "skip": 0}.get(CV, 31)):
                        self.op(PE, lambda e, blk=blk, j=j, yw=yw: e.matmul(
                            b6[:, blk * 128:(blk + 1) * 128], lhsT=yw[:, blk, 1 + j:129 + j],
                            rhs=W["d31"][:, blk, j, :], start=(j == 0), stop=False), reads=[r_yw, W["r_d31"]],
                            writes=[B["r_Y0"]], flag=False)
                    if CV in ("nobias", "one", "skip"):
                        continue
                    self.op(PE, lambda e, blk=blk: e.matmul(b6[:, blk * 128:(blk + 1) * 128],
                                                            lhsT=self.onesf[0:1, 0:128],
                                                            rhs=W["convb"][0:1, blk * 128:(blk + 1) * 128], start=False,
                                                            stop=True), reads=[self.r_onesf, W["r_convb"]],
                            writes=[B["r_Y0"]], flag=(blk == 1))
                self.chk("b61")
                rs_, nb_, r_mvc = self.ln_stats(b6[:, 0:256], 256, B["r_Y0"])
                cn, r_cn, _ = B["cn"].next()
                self.op(ACT, lambda e, cn=cn, rs_=rs_, nb_=nb_: e.activation(out=cn[:], in_=b6[:, 0:256],
                                                                            func=AF.Identity, scale=rs_, bias=nb_),
                        reads=[B["r_Y0"], r_mvc], writes=[r_cn])
                self.chk("b62")
                yield
                for blk in range(2):
                    self.op(PE, lambda e, blk=blk, cn=cn: e.transpose(b5[:, 256 + blk * 128:384 + blk * 128],
                                                                      cn[:, blk * 128:(blk + 1) * 128], self.identb[:]),
                            reads=[r_cn, self.r_identb], writes=[B["r_tr"]], flag=(blk == 1))
                for blk in range(2):
                    self.op(ACT, lambda e, blk=blk, yT=yT: e.activation(
                        out=yT[:, 6 + blk, :], in_=b5[:, 256 + blk * 128:384 + blk * 128], func=AF.Silu,
                        scale=fvec[:, 4 + blk:5 + blk], bias=fvec[:, 6 + blk:7 + blk]), reads=[B["r_tr"], r_fvec],
                        writes=[r_yT])
                self.chk("b7")
                yield
                if "YD" in DEBUG and sq == "lat":
                    self.op(SP, lambda e, yT=yT, i=i: e.dma_start(
                        out=self.YD[:, :, i * 128:(i + 1) * 128].rearrange("k p t -> p k t"), in_=yT[:]),
                        reads=[r_yT], writes=[self.r_YD], dsem=self.ydsem)
                Y = [b6, b7]
                r_Y = [B["r_Y0"], B["r_Y1"]]
                for nh in range(2):
                    for kc in range(8):
                        self.op(PE, lambda e, nh=nh, kc=kc, yT=yT: e.matmul(
                            Y[nh][:, :], lhsT=yT[:, kc, :], rhs=W["wout"][:, kc, nh * 512:(nh + 1) * 512],
                            start=(kc == 0), stop=(kc == 7)), reads=[r_yT, W["r_wout"]], writes=[r_Y[nh]],
                            flag=(kc == 7))
                self.chk("b8")
                t1, r_t1, _ = B["t1"].next()
                for nh in range(2):
                    self.op(DVE, lambda e, nh=nh, t1=t1: e.tensor_tensor(
                        out=t1[:, nh * 512:(nh + 1) * 512], in0=Y[nh][:, :], in1=self.grow[:, si, 0, nh * 512:(nh + 1) * 512],
                        op=ALU.mult), reads=[r_Y[nh], self.r_grow], writes=[r_t1])
                self.op(DVE, lambda e, t1=t1, xt=xt: e.scalar_tensor_tensor(out=t1[:], in0=xt[:], scalar=ALPHA, in1=t1[:],
                                                                             op0=ALU.mult, op1=ALU.add),
                        reads=[r_xt, r_t1], writes=[r_t1])
                self.chk("b9")
                yield
                yield from self.residual_tail(B, W, t1, r_t1, 0, si, sq, i, mod=True)
                self.chk("b10")
                self.chk("T:%s:%d" % (sq, i))

        self.pipeline(nt, prefB, tileB, depth=PIPE, lag=LAG_B)

    def residual_tail(self, B, W, t1, r_t1, which, si, sq, i, mod, dst=None, rdst=None):
        prow, r_prow = W["prow"], W["r_prow"]
        rs_, nb_, r_mv = self.ln_stats(t1, D, r_t1)
        x1, r_x1, d_x1 = B["x1"].next()
        self.op(ACT, lambda e: e.activation(out=x1[:], in_=t1[:], func=AF.Identity, scale=rs_, bias=nb_),
                reads=[r_t1, r_mv], writes=[r_x1])
        self.op(POOL, lambda e: e.tensor_tensor(out=x1[:], in0=x1[:], in1=prow[:, 2 * which, :], op=ALU.mult),
                reads=[r_x1, r_prow], writes=[r_x1])
        self.op(DVE, lambda e: e.tensor_tensor(out=x1[:], in0=x1[:], in1=prow[:, 2 * which + 1, :], op=ALU.add),
                reads=[r_x1, r_prow], writes=[r_x1])
        if dst is None:
            dst, rdst = self.X1[sq], self.r_dram["X1" + sq]
        self.op(SP, lambda e: e.dma_start(out=dst[i * 128:(i + 1) * 128, :], in_=x1[:]), reads=[r_x1],
                writes=[rdst], dsem=d_x1)
        if not mod:
            return
        yield
        rs2, nb2, r_mv2 = self.ln_stats(x1, D, r_x1)
        hn, r_hn, _ = B["hn"].next()
        self.op(ACT, lambda e: e.activation(out=hn[:], in_=x1[:], func=AF.Identity, scale=rs2, bias=nb2),
                reads=[r_x1, r_mv2], writes=[r_hn])
        yield
        b5 = B["b5"]
        for kc in range(8):
            self.op(PE, lambda e, kc=kc: e.transpose(b5[:, kc * 128:(kc + 1) * 128], hn[:, kc * 128:(kc + 1) * 128],
                                                     self.identb[:]), reads=[r_hn, self.r_identb],
                    writes=[B["r_tr"]], flag=(kc == 7))
        h2, r_h2, d_h2 = B["h2"].next()
        for kc in range(8):
            if kc % 2 == 0:
                self.op(ACT, lambda e, kc=kc: e.activation(
                    out=h2[:, kc, :], in_=b5[:, kc * 128:(kc + 1) * 128], func=AF.Identity,
                    scale=self.mT[:, si, 4, kc:kc + 1], bias=self.mT[:, si, 3, kc:kc + 1]),
                    reads=[B["r_tr"], self.r_mT], writes=[r_h2])
            else:
                self.op(DVE, lambda e, kc=kc: e.tensor_scalar(
                    out=h2[:, kc, :], in0=b5[:, kc * 128:(kc + 1) * 128], scalar1=self.mT[:, si, 4, kc:kc + 1],
                    scalar2=self.mT[:, si, 3, kc:kc + 1], op0=ALU.mult, op1=ALU.add),
                    reads=[B["r_tr"], self.r_mT], writes=[r_h2])
        self.op(SP, lambda e: e.dma_start(out=self.H2T[sq][:, :, i * 128:(i + 1) * 128].rearrange("k p t -> p k t"),
                                          in_=h2[:]), reads=[r_h2], writes=[self.r_dram["H2" + sq]], dsem=d_h2)

    def bufsC(self, pre):
        B = {}
        p2 = pre + "C_"
        RL, WL = self.R, GW
        wmax = max((RL + 2) * WL, LC)
        B["hw"] = Ring(self, p2 + "hw", [128, 8, wmax], BF16, 2, dma=True)
        B["wu"] = Ring(self, p2 + "wu", [128, 8, 256], BF16, 3, dma=True)
        B["dg"] = Ring(self, p2 + "dg", [128, 2, 9, 128], BF16, 2)
        B["U"] = Ring(self, p2 + "U", [128, max((RL + 2) * (WL + 2), LC + 2)], BF16, 4)
        B["sgl"] = Ring(self, p2 + "sgl", [128, max(RL * WL, LC)], F32, 2)
        B["act"] = Ring(self, p2 + "act", [128, NPAIR, max(RL * WL, LC)], BF16, 1)
        B["x1t"] = Ring(self, p2 + "x1t", [128, D], F32, 2, dma=True)
        B["t1"] = Ring(self, p2 + "t1", [128, D], F32, 1)
        B["x1"] = Ring(self, p2 + "x2", [128, D], F32, 2, dma=True)
        B["up"] = Ring(self, p2 + "up", [128, 512], F32, 4, psum=True)
        B["cv"] = Ring(self, p2 + "cv", [128, 512], F32, 2, psum=True)
        B["F"] = Ring(self, p2 + "F", [128, 512], F32, 2, psum=True)
        return B

    def passC(self, B, W, sq, xdst, rows, Wd, R):
        si = 0 if sq == "lat" else 1
        for t_, r_ in zip(B["U"].t, B["U"].r):
            self.op(POOL, lambda e, t_=t_: e.memset(t_[:], 0.0), writes=[r_])
        Ts = R * Wd
        nst = rows // R
        H2, rH2 = self.H2T[sq], self.r_dram["H2" + sq]
        X1, rX1 = self.X1[sq], self.r_dram["X1" + sq]
        cw9, r_cw9 = W["cw9"], W["r_cw9"]
        drs = [0] if rows == 1 else [-1, 0, 1]
        Wp = Wd + 2
        hws = {}

        def load_hw(s):
            r0 = s * R
            lo, hi = max(r0 - 1, 0), min(r0 + R + 1, rows)
            hw, r_hw, d_hw = B["hw"].next()
            self.op(SP, lambda e: e.dma_start(
                out=hw[:, :, 0:(hi - lo) * Wd], in_=H2[:, :, lo * Wd:hi * Wd].rearrange("k p t -> p k t")),
                reads=[rH2], writes=[r_hw], dsem=d_hw)
            hws[s] = (hw, r_hw)

        wus = {}

        def load_wu(key, j):
            wu, r_wu, d_wu = B["wu"].next()
            self.op(SP, lambda e: e.dma_start(out=wu[:].rearrange("p a b -> p (a b)"), in_=self.WUPB[j]),
                    reads=[self.r_dram["WUPB"]], writes=[r_wu], dsem=d_wu)
            wus[key] = (wu, r_wu)

        load_hw(0)
        for s in range(nst):
            r0 = s * R
            lo, hi = max(r0 - 1, 0), min(r0 + R + 1, rows)
            nrow = hi - lo
            boff = lo - (r0 - 1)
            hw, r_hw = hws.pop(s)
            act, r_act, _ = B["act"].next()
            rpg = max(1, 512 // Wd)
            groups = []
            a = 0
            ng = (nrow + rpg - 1) // rpg
            per = (nrow + ng - 1) // ng
            while a < nrow:
                groups.append((a, min(a + per, nrow)))
                a += per
            dgs = {}

            def build_dg(jj):
                dg, r_dg, _ = B["dg"].next()
                for half in range(2):
                    blk = jj + NPAIR * half
                    for dr in drs:
                        for dc in (-1, 0, 1):
                            tap = (dr + 1) * 3 + (dc + 1)
                            self.op(DVE, lambda e, dg=dg, half=half, tap=tap, blk=blk: e.tensor_scalar(
                                out=dg[:, half, tap, :], in0=self.identb[:], scalar1=cw9[:, blk, tap:tap + 1],
                                scalar2=None, op0=ALU.mult), reads=[self.r_identb, r_cw9], writes=[r_dg])
                return dg, r_dg

            for j in range(NPAIR):
                if (s, j) not in wus:
                    load_wu((s, j), j)
                wu, r_wu = wus.pop((s, j))
                if j == 0:
                    dgs[0] = build_dg(0)
                dg, r_dg = dgs.pop(j)
                cvs = []
                Us = []
                for half in range(2):
                    U, r_U, _ = B["U"].next()
                    Uv = U[:, 0:(R + 2) * Wp].rearrange("p (r w) -> p r w", w=Wp) if rows > 1 else None
                    Us.append((U, r_U, Uv))
                    if rows > 1 and r0 == 0:
                        self.op(POOL, lambda e, Uv=Uv: e.memset(Uv[:, 0, :], 0.0), writes=[r_U])
                    if rows > 1 and hi == rows and r0 + R + 1 > rows:
                        self.op(POOL, lambda e, Uv=Uv: e.memset(Uv[:, R + 1, :], 0.0), writes=[r_U])
                    for gi, (ga, gb) in enumerate(groups):
                        up, r_up, _ = B["up"].next()
                        ntok = (gb - ga) * Wd
                        for kc in range(8):
                            self.op(PE, lambda e, up=up, kc=kc, half=half, ga=ga, ntok=ntok, wu=wu, hw=hw: e.matmul(
                                up[:, 0:ntok], lhsT=wu[:, kc, half * 128:(half + 1) * 128],
                                rhs=hw[:, kc, ga * Wd:ga * Wd + ntok], start=(kc == 0), stop=(kc == 7)),
                                reads=[r_wu, r_hw], writes=[r_up], flag=(kc == 7))
                        b_a = boff + ga if rows > 1 else 1
                        dstv = Uv[:, b_a:b_a + (gb - ga), 1:1 + Wd] if rows > 1 else U[:, 1:1 + Wd]
                        srcv = up[:, 0:ntok].rearrange("p (r w) -> p r w", w=Wd) if rows > 1 else up[:, 0:ntok]
                        self.op(ACT, lambda e, dstv=dstv, srcv=srcv: e.activation(out=dstv, in_=srcv, func=AF.Copy),
                                reads=[r_up], writes=[r_U])
                if j + 1 < NPAIR:
                    dgs[j + 1] = build_dg(j + 1)
                for half in range(2):
                    U, r_U, Uv = Us[half]
                    cv, r_cv, _ = B["cv"].next()
                    taps = [(dr, dc) for dr in drs for dc in (-1, 0, 1)]
                    for ti, (dr, dc) in enumerate(taps):
                        tap = (dr + 1) * 3 + (dc + 1)
                        if rows > 1:
                            rhs = Uv[:, 1 + dr:1 + dr + R, 1 + dc:1 + dc + Wd]
                            outv = cv[:, 0:Ts].rearrange("p (r w) -> p r w", w=Wd)
                        else:
                            rhs = U[:, 1 + dc:1 + dc + Wd]
                            outv = cv[:, 0:Ts]
                        self.op(PE, lambda e, outv=outv, rhs=rhs, dg=dg, half=half, tap=tap, ti=ti, nt_=len(taps): e.matmul(
                            outv, lhsT=dg[:, half, tap, :], rhs=rhs, start=(ti == 0), stop=(ti == nt_ - 1)),
                            reads=[r_dg, r_U], writes=[r_cv], flag=(ti == len(taps) - 1))
                    cvs.append((cv, r_cv))
                sgl, r_sgl, _ = B["sgl"].next()
                self.op(ACT, lambda e, sgl=sgl, cv=cvs[1][0]: e.activation(out=sgl[:, 0:Ts], in_=cv[:, 0:Ts],
                                                                           func=AF.Silu), reads=[cvs[1][1]],
                        writes=[r_sgl])
                self.op(DVE, lambda e, sgl=sgl, cv=cvs[0][0], act=act, j=j: e.tensor_tensor(
                    out=act[:, j, 0:Ts], in0=cv[:, 0:Ts], in1=sgl[:, 0:Ts], op=ALU.mult), reads=[cvs[0][1], r_sgl],
                    writes=[r_act])
            if s + 1 < nst:
                load_hw(s + 1)
                for jj in range(2):
                    load_wu((s + 1, jj), jj)
            for ts in range(Ts // 128):
                ti = (r0 * Wd) // 128 + ts
                x1t, r_x1t, d_x1t = B["x1t"].next()
                self.op(SP, lambda e, x1t=x1t, ti=ti: e.dma_start(out=x1t[:], in_=X1[ti * 128:(ti + 1) * 128, :]),
                        reads=[rX1], writes=[r_x1t], dsem=d_x1t)
                t1, r_t1, _ = B["t1"].next()
                for nh in range(2):
                    Fp, r_F, _ = B["F"].next()
                    for j in range(NPAIR):
                        self.op(PE, lambda e, Fp=Fp, j=j, nh=nh, ts=ts, act=act: e.matmul(
                            Fp[:, :], lhsT=act[:, j, ts * 128:(ts + 1) * 128],
                            rhs=W["wdn"][:, j, nh * 512:(nh + 1) * 512], start=(j == 0), stop=(j == NPAIR - 1)),
                            reads=[r_act, W["r_wdn"]], writes=[r_F], flag=(j == NPAIR - 1))
                    self.op(DVE, lambda e, Fp=Fp, nh=nh, t1=t1: e.tensor_tensor(
                        out=t1[:, nh * 512:(nh + 1) * 512], in0=Fp[:, :],
                        in1=self.grow[:, si, 1, nh * 512:(nh + 1) * 512], op=ALU.mult), reads=[r_F, self.r_grow],
                        writes=[r_t1])
                self.op(DVE, lambda e, t1=t1, x1t=x1t: e.scalar_tensor_tensor(
                    out=t1[:], in0=x1t[:], scalar=ALPHA, in1=t1[:], op0=ALU.mult, op1=ALU.add), reads=[r_x1t, r_t1],
                    writes=[r_t1])
                rd = self.r_out if xdst is self.out else (self.r_dram["XS"] if sq == "lat" else self.r_dram["CS"])
                for _ in self.residual_tail(B, W, t1, r_t1, 1, si, sq, ti, mod=False, dst=xdst, rdst=rd):
                    pass


def host_consts():
    s = np.arange(128)[:, None]
    t = np.arange(128)[None, :]
    same = (s // 64) == (t // 64)
    mcum_f = (same & (s <= t)).astype(np.float32)
    mcum_b = (same & (s >= t)).astype(np.float32)
    mrev_f = (same & (s > t)).astype(np.float32)
    mrev_b = (same & (s < t)).astype(np.float32)
    mcum = np.stack([np.tile(mcum_f, (1, 4)), np.tile(mcum_b, (1, 4))], axis=1)
    mrev = np.stack([mrev_f, mrev_b], axis=1)
    ind = np.zeros((128, 64), np.float32)
    ind[0:64, 0] = 1.0
    ind[64:128, 1] = 1.0
    hm = (np.arange(128)[:, None] // 32 == np.arange(4)[None, :]).astype(np.float32)
    bm1 = (np.arange(128)[:, None] // 32 == (np.arange(256)[None, :] // 64)).astype(np.float32)
    bm = np.tile(bm1, (1, 2))
    return dict(identf=np.eye(128, dtype=np.float32), mcum=np.ascontiguousarray(mcum),
                mrev=np.ascontiguousarray(mrev), ind=ind, hm=hm, bm=np.ascontiguousarray(bm),
                onesf=np.ones((128, 128), np.float32))


def pool_consts(Ls_list):
    wins = (2, 4, 8, 16)
    pbm = np.zeros((128, 3, 4, 128), np.float32)
    pbp = np.zeros((128, 4, 128), np.float32)
    pbn = np.zeros((128, 4, 128), np.float32)
    Lbig = 128 * 5
    for gi, w in enumerate(wins):
        for var, tile in ((0, 0), (1, 2), (2, 4)):
            for t in range(128):
                tg = tile * 128 + t
                lo, hi = max(tg - w // 2, 0), min(tg + w - w // 2, Lbig)
                for sg in range(lo, hi):
                    sl = sg - tile * 128
                    val = 1.0 / (hi - lo)
                    if 0 <= sl < 128:
                        pbm[sl, var, gi, t] += val
                    elif var == 1 and sl < 0:
                        pbp[128 + sl, gi, t] += val
                    elif var == 1 and sl >= 128:
                        pbn[sl - 128, gi, t] += val
                pbm[t, var, gi, t] -= 1.0
    return dict(pbm=pbm, pbp=pbp, pbn=pbn)


def layer_inputs(l, p):
    f = lambda a: np.ascontiguousarray(a, dtype=np.float32)
    pre = "l%d_" % l
    o = {}
    wm = p["w_mod"][l].reshape(8, 128, 6, D)
    o["wmod"] = f(wm.transpose(2, 1, 0, 3).reshape(6, 128, 8 * D))
    bm_ = p["b_mod"][l].reshape(6, 8, 128)
    o["bmodT"] = f(bm_.transpose(2, 0, 1))
    o["brow"] = f(np.broadcast_to(np.stack([bm_[2].reshape(D), bm_[5].reshape(D)])[None], (128, 2, D)))
    wi = p["w_in"][l]
    CA = 800
    q, k, v, r, low = wi[:, 0:128], wi[:, 128:256], wi[:, 256:512], wi[:, 512:768], wi[:, 768:800]
    zb = wi[:, CA:CA + 256]
    zcu, zcv = wi[:, CA + 256:CA + 512], wi[:, CA + 512:CA + 768]
    za, zg = wi[:, CA + 768:CA + 1024], wi[:, CA + 1024:CA + 1280]
    wcat = np.concatenate([k, v, zb, zcv, q, k, r, zcu, za, zg, low], axis=1)
    assert wcat.shape[1] == NWIN
    o["win"] = f(wcat.reshape(8, 128, NWIN).transpose(1, 0, 2))
    wg = np.zeros((33, 256), np.float32)
    wg[0:16, 0:128] = p["gla_w_gate"][l, 0]
    wg[16:32, 128:256] = p["gla_w_gate"][l, 1]
    wg[32, 0:128] = p["gla_b_gate"][l, 0]
    wg[32, 128:256] = p["gla_b_gate"][l, 1]
    o["wg"] = wg
    o["wout"] = f(p["w_out"][l].reshape(8, 128, D).transpose(1, 0, 2))
    o["wdn"] = f(p["ffn_w_down"][l].reshape(NPAIR, 128, D).transpose(1, 0, 2))
    pw = np.zeros((128, 2, 128), np.float32)
    for g in range(4):
        a = (g % 2) * 64
        pw[a:a + 64, g // 2, a:a + 64] = p["pool_w"][l, g]
    o["pw"] = pw
    o["sw"] = f(p["sgu_w"][l].transpose(2, 0, 1))
    o["sgub"] = f(p["sgu_b"][l].reshape(1, 512))
    o["convb"] = f(p["cm_conv_b"][l].reshape(1, 256))
    o["lnrow"] = f(np.broadcast_to(np.stack([p["sgu_ln_w"][l], p["sgu_ln_b"][l]])[None], (128, 2, 256)))
    pr = np.stack([p["post_ln_w"][l, 0], p["post_ln_b"][l, 0], p["post_ln_w"][l, 1], p["post_ln_b"][l, 1]])
    o["prow"] = f(np.broadcast_to(pr[None], (128, 4, D)))
    fv = np.zeros((128, 8), np.float32)
    for i_, nm in enumerate(("gla_norm_w", "pool_scale", "cm_ln_w", "cm_ln_b")):
        fv[:, 2 * i_:2 * i_ + 2] = p[nm][l].reshape(2, 128).T
    o["fvec"] = fv
    o["cw31"] = f(p["cm_conv_w"][l].reshape(31, 2, 128).transpose(2, 1, 0))
    o["cw9"] = f(p["ffn_conv_w"][l].reshape(9, 44, 128).transpose(2, 1, 0))
    wu = p["ffn_w_up"][l].reshape(8, 128, 2, NPAIR, 128)
    o["wup"] = f(wu.transpose(3, 1, 0, 2, 4).reshape(NPAIR, 128, 8 * 256))
    return {pre + k_: v_ for k_, v_ in o.items()}


PIPE = 2
LAG_A = 0
LAG_B = 5
_CACHE = {}
STOP = None
DEBUG = set()
LAST = {}


def run(inputs, L, ncores, bidx):
    key = (L,)
    if key not in _CACHE:
        kb = K(L)
        kb.stop = STOP
        kb.chain_i = 0
        kb.r_out = Res()
        nc = kb.build()
        _CACHE[key] = (kb, nc)
    kb, nc = _CACHE[key]
    p = {k_: np.asarray(v_, dtype=np.float32) for k_, v_ in inputs.items()}
    shared = {}
    shared.update(host_consts())
    shared.update(pool_consts(None))
    for l in range(DEPTH):
        shared.update(layer_inputs(l, p))
    in_maps = []
    for ci in range(ncores):
        b = bidx[ci]
        m = dict(shared)
        m["x"] = np.ascontiguousarray(p["x"][b])
        m["ctx"] = np.ascontiguousarray(p["ctx"][b])
        cT = np.stack([p["c"][b].reshape(8, 128).T, p["c_ctx"].reshape(8, 128).T], axis=-1)
        m["cT"] = np.ascontiguousarray(cT, dtype=np.float32)
        in_maps.append(m)
    res = run_bass_kernel_spmd(nc, in_maps, core_ids=list(range(ncores)))
    if DEBUG:
        LAST["res"] = res.results
    return [np.asarray(r["out"]) for r in res.results]


def kernel(**inputs):
    x = np.asarray(inputs["x"])
    Bn, L, _ = x.shape
    bidx = [i % Bn for i in range(8)]
    outs = run(inputs, L, 8, bidx)
    return np.stack(outs[:Bn], axis=0).astype(np.float32)
```
